# Optimizing a Trainium2 kernel written in Bass

```python
import math
import jax, jax.numpy as jnp
from jax import lax
import numpy as np

D_MODEL = 1024
BATCH = 32
SEQ = 2048
DEPTH = 1
DEC_BATCH = 16
DEC_SEQ = 16
PAST_LEN = 2048

CHUNK = 64
A_HEADS = 4
A_DK = 128
A_DV = 128
A_WIDTH = A_HEADS * A_DV
CONV_W = 4
B_HEADS = 8
B_DH = 64
B_WIDTH = B_HEADS * B_DH
BAND_CHUNKS = 8
BAND_PAST = BAND_CHUNKS * CHUNK
REL_MAX = 4 * CHUNK
REL_SIZE = REL_MAX + CHUNK
MIX_WIDTH = A_WIDTH + B_WIDTH
D_FF = 4 * D_MODEL
EPS = 1e-6

QA = 0
KA = QA + A_HEADS * A_DK
VA = KA + A_HEADS * A_DK
GA = VA + A_WIDTH
BA = GA + A_WIDTH
AA = BA + A_HEADS
QB = AA + A_HEADS
KB = QB + B_WIDTH
VB = KB + B_WIDTH
IN_COLS = VB + B_WIDTH
CONV_CH = GA - QA

kernel_name = "hybrid_gdn_chunkband_stream_step"


def rmsnorm(x, g):
    xf = x.astype(jnp.float32)
    y = xf * lax.rsqrt(jnp.mean(xf * xf, axis=-1, keepdims=True) + EPS)
    return (y * g.astype(jnp.float32)).astype(x.dtype)


def l2norm(x):
    xf = x.astype(jnp.float32)
    return xf * lax.rsqrt(jnp.sum(xf * xf, axis=-1, keepdims=True) + EPS)


def modulation(c, w_mod, b_mod):
    m = jax.nn.silu(c) @ w_mod + b_mod
    return jnp.split(m[:, None, :], 6, axis=-1)


def causal_conv_silu(u, left, w):
    full = jnp.concatenate([left.astype(u.dtype), u], axis=1)
    T = u.shape[1]
    out = full[:, 0:T] * w[0]
    for i in range(1, CONV_W):
        out = out + full[:, i:i + T] * w[i]
    return jax.nn.silu(out), full[:, full.shape[1] - (CONV_W - 1):]


def gated_delta_chunked(q, k, v, beta, g, s0):
    B, T, H, DK = q.shape
    DV = v.shape[-1]
    L = min(CHUNK, T)
    n = T // L

    def blk(x):
        x = x.astype(jnp.float32).reshape((B, n, L, H) + x.shape[3:])
        return jnp.moveaxis(x, (1, 3), (0, 2))

    q, k, v, beta, g = blk(q), blk(k), blk(v), blk(beta), blk(g)
    gc = jnp.cumsum(g, axis=-1)
    idx = jnp.arange(L)
    incl = idx[:, None] >= idx[None, :]
    strict = idx[:, None] > idx[None, :]
    decay = jnp.exp(jnp.where(incl, gc[..., :, None] - gc[..., None, :], -jnp.inf))
    kb = k * beta[..., None]
    m = jnp.where(strict, jnp.einsum('nbhid,nbhjd->nbhij', kb, k) * decay, 0.0)
    a = jnp.eye(L, dtype=jnp.float32) + m
    rhs = jnp.concatenate([v * beta[..., None], kb * jnp.exp(gc)[..., None]], axis=-1)
    sol = lax.linalg.triangular_solve(a, rhs, left_side=True, lower=True, unit_diagonal=True)
    u, w = sol[..., :DV], sol[..., DV:]
    qk = jnp.einsum('nbhid,nbhjd->nbhij', q, k) * decay

    def step(s, inp):
        qc, kc, uc, wc, qkc, gcc = inp
        vnew = uc - jnp.einsum('bhld,bhde->bhle', wc, s)
        o = (jnp.einsum('bhld,bhde->bhle', qc * jnp.exp(gcc)[..., None], s)
             + jnp.einsum('bhij,bhje->bhie', qkc, vnew))
        glast = gcc[..., -1]
        s = (s * jnp.exp(glast)[..., None, None]
             + jnp.einsum('bhld,bhle->bhde', kc * jnp.exp(glast[..., None] - gcc)[..., None], vnew))
        return s, o

    s, o = lax.scan(step, s0.astype(jnp.float32), (q, k, u, w, qk, gc))
    o = jnp.moveaxis(o, (0, 2), (1, 3)).reshape(B, T, H, DV)
    return o, s


def mixer_a(proj, conv_left, s0, conv_w, a_log, dt_bias, gdn_norm_g):
    B, T, _ = proj.shape
    qkv, conv_state = causal_conv_silu(proj[..., QA:GA], conv_left, conv_w)
    q = l2norm(qkv[..., QA:KA].reshape(B, T, A_HEADS, A_DK)) * (A_DK ** -0.5)
    k = l2norm(qkv[..., KA:VA].reshape(B, T, A_HEADS, A_DK))
    v = qkv[..., VA:GA].reshape(B, T, A_HEADS, A_DV)
    gate = proj[..., GA:BA].astype(jnp.float32).reshape(B, T, A_HEADS, A_DV)
    beta = jax.nn.sigmoid(proj[..., BA:AA].astype(jnp.float32))
    g = -jnp.exp(a_log.astype(jnp.float32)) * jax.nn.softplus(
        proj[..., AA:QB].astype(jnp.float32) + dt_bias.astype(jnp.float32))
    o, s = gated_delta_chunked(q, k, v, beta, g, s0)
    o = rmsnorm(o, gdn_norm_g) * jax.nn.silu(gate)
    return o.reshape(B, T, A_WIDTH).astype(proj.dtype), conv_state, s


def rel_bias_matrix(rel_bias, qpos, kpos):
    d = jnp.clip(qpos[:, None] - kpos[None, :], -(CHUNK - 1), REL_MAX) + (CHUNK - 1)
    return rel_bias.astype(jnp.float32)[:, d]


def band_attention_prompt(q, k, v, rel_bias):
    B, T, H, DH = q.shape
    n = T // CHUNK
    band = BAND_PAST + CHUNK
    pad = ((0, 0), (BAND_PAST, 0), (0, 0), (0, 0))
    kp, vp = jnp.pad(k, pad), jnp.pad(v, pad)
    j = jnp.arange(band)
    bias = rel_bias_matrix(rel_bias, jnp.arange(CHUNK) + BAND_PAST, j)

    def one_chunk(c):
        start = c * CHUNK
        qc = lax.dynamic_slice_in_dim(q, start, CHUNK, axis=1)
        kc = lax.dynamic_slice_in_dim(kp, start, band, axis=1)
        vc = lax.dynamic_slice_in_dim(vp, start, band, axis=1)
        valid = (start - BAND_PAST + j) >= 0
        s = jnp.einsum('bqhd,bkhd->bhqk', qc, kc).astype(jnp.float32) * (DH ** -0.5) + bias
        p = jax.nn.softmax(jnp.where(valid, s, -jnp.inf), axis=-1).astype(v.dtype)
        return jnp.einsum('bhqk,bkhd->bqhd', p, vc)

    o = lax.map(one_chunk, jnp.arange(n))
    return jnp.moveaxis(o, 0, 1).reshape(B, T, H * DH)


def band_attention_sample(q, k_new, v_new, k_cache, v_cache, rel_bias):
    B, T, H, DH = q.shape
    Lc = k_cache.shape[1]
    kk = jnp.concatenate([k_cache.astype(k_new.dtype), k_new], axis=1)
    vv = jnp.concatenate([v_cache.astype(v_new.dtype), v_new], axis=1)
    bias = rel_bias_matrix(rel_bias, jnp.arange(T) + Lc, jnp.arange(Lc + T))
    s = jnp.einsum('bqhd,bkhd->bhqk', q, kk).astype(jnp.float32) * (DH ** -0.5) + bias
    p = jax.nn.softmax(s, axis=-1).astype(vv.dtype)
    return jnp.einsum('bhqk,bkhd->bqhd', p, vv).reshape(B, T, H * DH)


def trunk_layer(x, c, conv_left, s0, band_cache, w_mod, b_mod, norm1_g, norm2_g, w_in, conv_w,
                a_log, dt_bias, gdn_norm_g, qn_g, kn_g, rel_bias, w_out, w_up, w_down):
    B, T, _ = x.shape
    sh1, sc1, g1, sh2, sc2, g2 = modulation(c, w_mod, b_mod)
    h = rmsnorm(x, norm1_g) * (1 + sc1) + sh1
    proj = h @ w_in
    if conv_left is None:
        conv_left = jnp.zeros((B, CONV_W - 1, CONV_CH), proj.dtype)
        s0 = jnp.zeros((B, A_HEADS, A_DK, A_DV), jnp.float32)
    o_a, conv_state, s = mixer_a(proj, conv_left, s0, conv_w, a_log, dt_bias, gdn_norm_g)
    qb = rmsnorm(proj[..., QB:KB].reshape(B, T, B_HEADS, B_DH), qn_g)
    kb = rmsnorm(proj[..., KB:VB].reshape(B, T, B_HEADS, B_DH), kn_g)
    vb = proj[..., VB:IN_COLS].reshape(B, T, B_HEADS, B_DH)
    if band_cache is None:
        o_b = band_attention_prompt(qb, kb, vb, rel_bias)
        keep = min(BAND_PAST, T)
        k_state, v_state = kb[:, T - keep:], vb[:, T - keep:]
    else:
        o_b = band_attention_sample(qb, kb, vb, band_cache[0], band_cache[1], rel_bias)
        k_state, v_state = kb, vb
    x = x + g1 * (jnp.concatenate([o_a, o_b], axis=-1) @ w_out)
    h = rmsnorm(x, norm2_g) * (1 + sc2) + sh2
    x = x + g2 * (jnp.square(jax.nn.relu(h @ w_up)) @ w_down)
    return x, conv_state, s, k_state, v_state


def setup_inputs(seed: int = 0) -> dict:
    key = jax.random.key(seed)
    ks = jax.random.split(key, 24)
    f32 = jnp.float32

    def nrm(k, shape, s):
        return jax.random.normal(k, shape, f32) * s

    band_len = min(BAND_PAST, PAST_LEN)
    dt = jnp.exp(jax.random.uniform(ks[16], (DEPTH, A_HEADS), f32, math.log(1e-3), math.log(1e-1)))
    return {
        "x_prompt": nrm(ks[0], (BATCH, SEQ, D_MODEL), 1.0),
        "x_sample": nrm(ks[1], (DEC_BATCH, DEC_SEQ, D_MODEL), 1.0),
        "state_conv": nrm(ks[2], (DEPTH, DEC_BATCH, CONV_W - 1, CONV_CH), 1.0),
        "state_gdn": nrm(ks[3], (DEPTH, DEC_BATCH, A_HEADS, A_DK, A_DV), 0.5),
        "cache_k_band": nrm(ks[4], (DEPTH, DEC_BATCH, band_len, B_HEADS, B_DH), 1.0),
        "cache_v_band": nrm(ks[5], (DEPTH, DEC_BATCH, band_len, B_HEADS, B_DH), 1.0),
        "c_prompt": nrm(ks[6], (BATCH, D_MODEL), 1.0),
        "c_sample": nrm(ks[7], (DEC_BATCH, D_MODEL), 1.0),
        "w_mod": nrm(ks[8], (DEPTH, D_MODEL, 6 * D_MODEL), D_MODEL ** -0.5),
        "b_mod": nrm(ks[9], (DEPTH, 6 * D_MODEL), 0.01),
        "norm1_g": 1.0 + nrm(ks[10], (DEPTH, D_MODEL), 0.1),
        "norm2_g": 1.0 + nrm(ks[11], (DEPTH, D_MODEL), 0.1),
        "w_in": nrm(ks[12], (DEPTH, D_MODEL, IN_COLS), D_MODEL ** -0.5),
        "conv_w": nrm(ks[13], (DEPTH, CONV_W, CONV_CH), CONV_W ** -0.5),
        "a_log": jnp.log(jax.random.uniform(ks[14], (DEPTH, A_HEADS), f32, 1.0, 16.0)),
        "dt_bias": dt + jnp.log(-jnp.expm1(-dt)),
        "gdn_norm_g": 1.0 + nrm(ks[15], (DEPTH, A_DV), 0.1),
        "qn_g": 1.0 + nrm(ks[17], (DEPTH, B_DH), 0.1),
        "kn_g": 1.0 + nrm(ks[18], (DEPTH, B_DH), 0.1),
        "rel_bias": nrm(ks[19], (DEPTH, B_HEADS, REL_SIZE), 0.5),
        "w_out": nrm(ks[20], (DEPTH, MIX_WIDTH, D_MODEL), MIX_WIDTH ** -0.5),
        "w_up": nrm(ks[21], (DEPTH, D_MODEL, D_FF), D_MODEL ** -0.5),
        "w_down": nrm(ks[22], (DEPTH, D_FF, D_MODEL), D_FF ** -0.5),
    }


def reference(x_prompt, x_sample, state_conv, state_gdn, cache_k_band, cache_v_band, c_prompt, c_sample,
              w_mod, b_mod, norm1_g, norm2_g, w_in, conv_w, a_log, dt_bias, gdn_norm_g, qn_g, kn_g,
              rel_bias, w_out, w_up, w_down):
    yp, ys = x_prompt, x_sample
    cp_l, gp_l, kp_l, vp_l, cs_l, gs_l, ks_l, vs_l = [], [], [], [], [], [], [], []
    for l in range(DEPTH):
        wl = (w_mod[l], b_mod[l], norm1_g[l], norm2_g[l], w_in[l], conv_w[l], a_log[l], dt_bias[l],
              gdn_norm_g[l], qn_g[l], kn_g[l], rel_bias[l], w_out[l], w_up[l], w_down[l])
        yp, cp, gp, kp, vp = trunk_layer(yp, c_prompt, None, None, None, *wl)
        ys, cs, gs, ksm, vsm = trunk_layer(ys, c_sample, state_conv[l], state_gdn[l],
                                           (cache_k_band[l], cache_v_band[l]), *wl)
        cp_l.append(cp); gp_l.append(gp); kp_l.append(kp); vp_l.append(vp)
        cs_l.append(cs); gs_l.append(gs); ks_l.append(ksm); vs_l.append(vsm)
    conv_prompt = jnp.stack(cp_l)
    gdn_prompt = jnp.stack(gp_l)
    kband_prompt = jnp.stack(kp_l)
    vband_prompt = jnp.stack(vp_l)
    conv_sample = jnp.stack(cs_l)
    gdn_sample = jnp.stack(gs_l)
    knew_sample = jnp.stack(ks_l)
    vnew_sample = jnp.stack(vs_l)
    return (yp, ys, conv_prompt, gdn_prompt, kband_prompt, vband_prompt,
            conv_sample, gdn_sample, knew_sample, vnew_sample)
```

```python
from contextlib import ExitStack
import numpy as np
import concourse.bass as bass
import concourse.mybir as mybir
from concourse.ap import AP
from concourse.bass_utils import run_bass_kernel_spmd

F32 = mybir.dt.float32
BF16 = mybir.dt.bfloat16
AF = mybir.ActivationFunctionType
ALU = mybir.AluOpType
AX = mybir.AxisListType

D = 1024
NCORES = 8
IN_COLS = 3592
GA_, BA_, AA_, QB_, KB_, VB_ = 1536, 2048, 2052, 2056, 2568, 3080
EPS = 1e-6
RING = 5
NEG = -30000.0


class Buf:
    __slots__ = ("name", "writers", "readers", "slot", "excl")

    def __init__(self, name, excl=False):
        self.name = name
        self.writers = {}
        self.readers = {}
        self.slot = None
        self.excl = excl


class Slot:
    __slots__ = ("sem", "count", "name")

    def __init__(self, sem, name):
        self.sem = sem
        self.count = 0
        self.name = name


class Op:
    __slots__ = ("eng", "fn", "deps", "needs_inc", "slot", "count", "is_dma", "idx")

    def __init__(self, eng, fn, is_dma):
        self.eng = eng
        self.fn = fn
        self.deps = []
        self.needs_inc = False
        self.slot = None
        self.count = None
        self.is_dma = is_dma


ENGS = ("pe", "act", "dve", "pool", "sp")
ENGOBJ = {"pe": "tensor", "act": "scalar", "dve": "vector", "pool": "gpsimd", "sp": "sync"}


class Prog:
    def __init__(self, nc, stack):
        self.nc = nc
        self.stack = stack
        self.ops = {e: [] for e in ENGS}
        self.eng_slot = {e: self.new_slot("eng_" + e) for e in ENGS}
        self.nops = 0
        self.stores = []
        self.cut = False

    def new_slot(self, name):
        sem = self.stack.enter_context(self.nc.semaphore(name))
        return Slot(sem, name)

    def add(self, eng, fn, reads=(), writes=(), dma_buf=None, extra_deps=()):
        if self.cut:
            return None
        is_dma = dma_buf is not None
        op = Op(eng, fn, is_dma)
        op.idx = self.nops
        self.nops += 1
        deps = {}
        for b in reads:
            for d in b.writers.values():
                deps[id(d)] = d
            if b.excl:
                for d in b.readers.values():
                    if d.eng != eng:
                        deps[id(d)] = d
        for b in writes:
            for d in b.writers.values():
                deps[id(d)] = d
            for d in b.readers.values():
                deps[id(d)] = d
        for d in extra_deps:
            deps[id(d)] = d
        for d in deps.values():
            if d.eng == eng and eng == "pe" and not d.is_dma:
                continue
            op.deps.append(d)
            d.needs_inc = True
        key = ("dma", op.idx) if is_dma else eng
        for b in reads:
            b.readers[key] = op
        for b in writes:
            b.writers = {key: op}
            b.readers = {}
        if is_dma:
            if dma_buf.slot is None:
                dma_buf.slot = {}
            if eng not in dma_buf.slot:
                dma_buf.slot[eng] = self.new_slot("d%s_%s" % (eng, dma_buf.name))
            op.slot = dma_buf.slot[eng]
            op.slot.count += 16
            op.count = op.slot.count
            op.needs_inc = True
        else:
            op.slot = self.eng_slot[eng]
        self.ops[eng].append(op)
        return op

    def emit(self):
        nc = self.nc
        for e in ENGS:
            c = 0
            for op in self.ops[e]:
                if op.is_dma:
                    continue
                if op.needs_inc:
                    c += 1
                    op.count = c
        stats = {}
        with nc.Block() as blk:
            for e in ENGS:
                ops = self.ops[e]
                if not ops:
                    continue

                def body(eng, ops=ops, e=e):
                    waited = {}
                    nw = 0
                    for op in ops:
                        need = {}
                        for d in op.deps:
                            k = id(d.slot)
                            if waited.get(k, 0) >= d.count:
                                continue
                            if k not in need or need[k][1] < d.count:
                                need[k] = (d.slot, d.count)
                        for k, (slot, cnt) in need.items():
                            eng.wait_ge(slot.sem, cnt)
                            waited[k] = cnt
                            nw += 1
                        ins = op.fn(eng)
                        if op.needs_inc:
                            ins.then_inc(op.slot.sem, 16 if op.is_dma else 1)
                    stats[e] = (len(ops), nw)

                getattr(blk, ENGOBJ[e])(body)
        return stats


def ap3(t, part, dims, off=0):
    base = t[:]
    pstep = base.ap[0][0]
    return AP(base.tensor, base.offset + off, [[pstep, part]] + [list(d) for d in dims])


def build_nc(n_pseq=4, T=2048, n_sseq=2, dbg=False):
    NT = T // 128
    NB = n_pseq + n_sseq
    KEEP = min(512, T) // 128
    nc = bass.Bass("TRN2", target_bir_lowering=False)

    def din(name, shape, dt=F32):
        return nc.dram_tensor(name, list(shape), dt, kind="ExternalInput").ap()

    def dout(name, shape, dt=F32):
        return nc.dram_tensor(name, list(shape), dt, kind="ExternalOutput").ap()

    xp = din("xp", [n_pseq, T, D])
    xs = din("xs", [max(n_sseq, 1), 16, D])
    sconv = din("sconv", [max(n_sseq, 1), 3, 1536])
    sgdn = din("sgdn", [max(n_sseq, 1), 4, 128, 128])
    ckd = din("ck", [max(n_sseq, 1), 512, 512])
    cvd = din("cv", [max(n_sseq, 1), 512, 512])
    cT = din("cT", [128, 8, NB])
    w_mod = din("w_mod", [D, 6 * D])
    bmodT = din("bmodT", [128, 48])
    n1gT = din("n1gT", [128, 8])
    n2gT = din("n2gT", [128, 8])
    w_in = din("w_in", [D, IN_COLS])
    cwT = din("cwT", [128, 12, 4])
    alog = din("alog", [1, 4])
    dtb_d = din("dtb", [1, 4])
    gng_d = din("gng", [1, 128])
    qkg_d = din("qkg", [1, 128])
    biasd = din("biasT", [128, 5, 8, 128])
    maskd = din("maskT", [128, 5, 128])
    w_out = din("w_out", [D, D])
    w_up = din("w_up", [D, 4 * D])
    w_down = din("w_down", [4 * D, D])
    constd = din("consts", [128, 5, 128])
    validd = din("valid16", [128, 1])
    gmaskd = din("gmask", [128, 5, 128])

    y_p = dout("y_p", [n_pseq, T, D])
    y_s = dout("y_s", [max(n_sseq, 1), 16, D])
    conv_p = dout("conv_p", [n_pseq, 3, 1536])
    gdn_p = dout("gdn_p", [n_pseq, 4, 128, 128])
    kb_p = dout("kb_p", [n_pseq, KEEP * 128, 512])
    vb_p = dout("vb_p", [n_pseq, KEEP * 128, 512])
    conv_s = dout("conv_s", [max(n_sseq, 1), 3, 1536])
    gdn_s = dout("gdn_s", [max(n_sseq, 1), 4, 128, 128])
    kn_s = dout("kn_s", [max(n_sseq, 1), 16, 512])
    vn_s = dout("vn_s", [max(n_sseq, 1), 16, 512])

    wo_s = nc.dram_tensor("wo_s", [4, 128, 2048], BF16, kind="Internal").ap()
    wa_s = nc.dram_tensor("wa_s", [6, 128, 2048], BF16, kind="Internal").ap()
    wu_s = nc.dram_tensor("wu_s", [16, 128, 2048], BF16, kind="Internal").ap()
    wd_s = nc.dram_tensor("wd_s", [16, 128, 2048], BF16, kind="Internal").ap()

    st = ExitStack()
    with st:
        P = Prog(nc, st)

        def sb(name, shape, dt):
            return st.enter_context(nc.sbuf_tensor(name, list(shape), dt))

        def E(eng, meth, reads, writes, *a, **kw):
            return P.add(eng, lambda e: getattr(e, meth)(*a, **kw), reads=reads, writes=writes)

        def DMA(eng, out, in_, reads, writes, buf, **kw):
            return P.add(eng, lambda e: e.dma_start(out=out, in_=in_, **kw), reads=reads, writes=writes, dma_buf=buf)

        WB0 = GA_
        win_sb = sb("win_sb", [128, 8, IN_COLS - WB0], BF16); WIN = [Buf("win%d" % k) for k in range(8)]
        NWR = 3
        wring = [sb("wring%d" % i, [128, 2048], BF16) for i in range(NWR)]
        WRING = [Buf("wring%d" % i) for i in range(NWR)]
        xg = [sb("xg%d" % i, [128, 2, D], F32) for i in range(2)]
        XG = [[Buf("xg%d_%d" % (i, j)) for j in range(2)] for i in range(2)]
        hT = sb("hT", [128, 8, 256], BF16); HT = [Buf("hT%d" % j) for j in range(2)]
        h2T = sb("h2T", [128, 8, 256], BF16); H2T = [Buf("h2T%d" % j) for j in range(2)]
        ffTt = sb("ffTt", [128, 32 * 256], BF16); FFB = Buf("ffT")
        R1 = sb("R1", [128, 4644], F32)
        R1b = R1.bitcast(BF16)
        PCB = [Buf("pc%d" % j) for j in range(12)]
        QKB = [Buf("qk%d" % j) for j in range(12)]
        R1ALL = PCB + QKB

        def pc(j, c0, c1):
            return R1[:, j * 259 + c0: j * 259 + c1]

        def qkvT(j, c0, c1):
            return R1b[:, 6216 + j * 256 + c0: 6216 + j * 256 + c1]

        def ffT(f, c0, c1):
            return ffTt[:, f * 256 + c0: f * 256 + c1]

        cacc = [sb("cacc%d" % i, [128, 256], F32) for i in range(2)]; CACC = [Buf("cacc%d" % i) for i in range(2)]
        ctnh = [sb("ctnh%d" % i, [128, 256], F32) for i in range(2)]; CTNH = [Buf("ctnh%d" % i) for i in range(2)]
        sqT = sb("sqT", [128, 8, 256], BF16); SQT = Buf("sqT")
        bsb = sb("bsb", [128, 1536], F32); BSB = [Buf("bsb_q"), Buf("bsb_k"), Buf("bsb_v")]
        xn_bf = sb("xn_bf", [128, D], BF16); XN = Buf("xn")
        qkn_bf = sb("qkn_bf", [128, 1024], BF16); QKN = Buf("qkn")
        qTz = [sb("qTz%d" % i, [128, 4, 128], BF16) for i in range(2)]; QT = Buf("qT")
        kring = sb("kring", [128, 4, RING * 128], BF16); KRING = [Buf("kr%d" % i) for i in range(RING)]
        vring = sb("vring", [128, RING, 8, 65], BF16); VRING = [Buf("vr%d" % i) for i in range(RING)]
        gsil2 = [sb("gsil%d" % i, [128, 512], F32) for i in range(2)]; GSIL2 = [Buf("gsil%d" % i) for i in range(2)]
        sc2 = sb("sc2", [128, 4, 48], F32); SCA = [Buf("sca_%d" % i) for i in range(4)]; SCN = [Buf("scn_%d" % i) for i in range(4)]
        small = sb("small", [128, 128], F32); SMG = Buf("smallg"); SMA = Buf("smalla")
        smn = sb("smn", [128, 4], F32); SMN = Buf("smn")
        cctx = sb("cctx", [128, 12, 3], F32); CCTX = Buf("cctx")
        kn3 = sb("kn3", [128, 3, 512], BF16); KN3 = Buf("kn3")
        vbt = sb("vbt", [128, 512], BF16); VBT = Buf("vbt")
        knT = sb("knT", [128, 512], BF16); KNT = Buf("knT")
        em = sb("em", [128, 512], F32); ESB = Buf("Esb"); MBB = Buf("MB")
        Esb = em[:, 0:512]
        mbt = sb("mbt", [128, 512], BF16)
        MBt = mbt[:, 0:512]
        t1 = sb("t1", [128, 512], F32); T1 = Buf("t1")
        gU = sb("gU", [128, 512], F32); GU = Buf("gU")
        gbf = [sb("gbf%d" % i, [128, 512], BF16) for i in range(4)]; GBF = [Buf("gbf%d" % i) for i in range(4)]
        gbf.insert(2, gbf[3]); GBF.insert(2, GBF[3])
        gmb = sb("gmb", [128, 5, 128], BF16); GMB = Buf("gmb")
        NTb = [sb("NTb%d" % i, [128, 512], BF16) for i in range(2)]; NTB = [Buf("NTb%d" % i) for i in range(2)]
        NQ = [sb("NQ0", [128, 1024], BF16), sb("NQ1", [128, 512], BF16)]; NQB = [Buf("NQ%d" % i) for i in range(2)]
        qkb = sb("qkb", [128, 512], BF16); QKBF = Buf("qkb")
        Yb = [sb("Yb%d" % i, [128, 512], BF16) for i in range(2)]; YB = [Buf("Yb%d" % i) for i in range(2)]
        nwT = sb("nwT", [128, 512], BF16); NWT = Buf("nwT")
        vnew = sb("vnew", [128, 512], BF16); VNEW = Buf("vnew")
        S32 = sb("S32", [128, 512], F32); SB32 = Buf("S32")
        Sbf = sb("Sbf", [128, 512], BF16); SBF = Buf("Sbf")
        ot = sb("ot", [128, 512], F32); OT = Buf("ot")
        mixb2 = [sb("mix_bf%d" % i, [128, D], BF16) for i in range(2)]; MIXA2 = [Buf("mixA%d" % i) for i in range(2)]
        MIXB2 = [[Buf("mixB%d_%d" % (i, j)) for j in range(2)] for i in range(2)]
        mixT = sb("mixT", [128, 8, 256], BF16); MIXT = [Buf("mixT%d" % j) for j in range(2)]
        scb = [sb("scb%d" % i, [128, 512], F32) for i in range(2)]; SCB = [Buf("scb%d" % i) for i in range(2)]
        Pb0 = sb("Pb0", [128, 5, 512], BF16); Pb = [Pb0, Pb0]; PB0 = [Buf("Pb_%d" % d) for d in range(5)]; PB = [PB0, PB0]
        yT_sb = [sb("yT%d" % i, [128, 256], F32) for i in range(2)]; YT = [Buf("yT%d" % i) for i in range(2)]
        fft = [sb("fft%d" % i, [128, 256], F32) for i in range(2)]; FFT = [Buf("fft%d" % i) for i in range(2)]
        bias_sb = sb("bias_sb", [128, 5, 8, 128], BF16); BIAS = Buf("bias")
        cst = sb("cst", [128, 5, 128], F32); CST = Buf("cst")
        identb = sb("identb", [128, 128], BF16); IDB = Buf("identb")
        gng4 = sb("gng4", [128, 128], F32); GNG = Buf("gng4")
        qkg8 = sb("qkg8", [128, 128], F32); QKG = Buf("qkg8")
        misc = sb("misc", [128, 64], F32); MISC = Buf("misc")
        modT = sb("modT", [128, 48, NB], F32); MODT = Buf("modT")
        hsc = sb("hsc", [128, 2, 8, NB], F32); HSC = Buf("hsc")
        scT = sb("scT", [128, 8, NB], F32); SCT = Buf("scT")
        vecs = sb("vecs", [128, 48 + 16], F32); VECS = Buf("vecs")
        cw = sb("cw", [128, 12, 4], F32); CW = Buf("cw")

        MH = misc[:, 0:1]
        VAL = misc[:, 1:2]
        ident = cst[:, 0, :]
        UPi = cst[:, 1, :]
        LOWs = cst[:, 2, :]
        LOWi = cst[:, 3, :]
        ones = cst[:, 4, :]

        pbk = [st.enter_context(nc.psum_tensor("pb%d" % i, [128, 512], F32)) for i in range(8)]
        PBK = [[Buf("pb%d" % i, excl=True)] for i in range(8)]
        TR, PJ, GA, GB, GC, SC0, SC1, PV = range(8)
        pool_banks = [PJ, GA, GB, GC, SC0, SC1, PV]
        rr = {"full": 0, "half": 0}

        def next_full():
            b = pool_banks[rr["full"] % len(pool_banks)]
            rr["full"] += 1
            return b

        def next_half():
            i = rr["half"] % (2 * len(pool_banks))
            rr["half"] += 1
            return pool_banks[i // 2], i % 2

        sp_q = "sp"
        DMA(sp_q, cst[:], constd, [], [CST], CST)
        E("act", "copy", [CST], [IDB], out=identb[:], in_=ident)
        DMA(sp_q, misc[:, 1:2], validd, [], [MISC], MISC)
        for (c0, src) in ((4, dtb_d), (12, alog)):
            DMA(sp_q, misc[:, c0:c0 + 4], AP(src.tensor, src.offset, [[0, 128], [1, 4]]), [], [MISC], MISC)
        E("pool", "memset", [], [MISC], misc[:, 0:1], -0.5)
        E("act", "activation", [MISC], [MISC], out=misc[:, 8:12], in_=misc[:, 12:16], func=AF.Exp)
        E("dve", "tensor_scalar", [MISC], [MISC], out=misc[:, 8:12], in0=misc[:, 8:12], scalar1=-1.0, scalar2=None, op0=ALU.mult)
        DMA(sp_q, gng4[:, 0:128], AP(gng_d.tensor, gng_d.offset, [[0, 128], [1, 128]]), [], [GNG], GNG)
        E("dve", "tensor_scalar", [GNG], [GNG], out=gng4[:, 0:128], in0=gng4[:, 0:128], scalar1=0.5, scalar2=None, op0=ALU.mult)
        DMA(sp_q, qkg8[:, 0:128], AP(qkg_d.tensor, qkg_d.offset, [[0, 128], [1, 128]]), [], [QKG], QKG)
        DMA(sp_q, cw[:], cwT, [], [CW], CW)
        E("dve", "tensor_scalar", [CW], [CW], out=cw[:], in0=cw[:], scalar1=0.5, scalar2=None, op0=ALU.mult)
        P.add("pool", lambda e: e.dma_start(out=bias_sb[:], in_=biasd), writes=[BIAS], dma_buf=BIAS)
        DMA(sp_q, bsb[:, 0:640], maskd.rearrange("p a b -> p (a b)"), [], BSB, BSB[0])
        E("dve", "tensor_tensor", [BIAS] + BSB, [BIAS], out=bias_sb[:],
          in0=bias_sb[:], in1=ap3(bsb, 128, [[128, 5], [0, 8], [1, 128]]), op=ALU.add)
        P.add("pool", lambda e: e.dma_start(out=gmb[:], in_=gmaskd), writes=[GMB], dma_buf=GMB)
        for i in range(2):
            E("pool", "memset", [], [QT], qTz[i][:], 0.0)
        E("pool", "memset", [], VRING, vring[:], 1.0)
        DMA(sp_q, vecs[:, 0:48], bmodT, [], [VECS], VECS)
        DMA(sp_q, vecs[:, 48:56], n1gT, [], [VECS], VECS)
        DMA(sp_q, vecs[:, 56:64], n2gT, [], [VECS], VECS)
        DMA(sp_q, scT[:], cT, [], [SCT], SCT)
        scf = scT[:].rearrange("p a b -> p (a b)")
        E("act", "activation", [SCT], [SMG], out=small[:, 0:8 * NB], in_=scf, func=AF.Tanh, scale=0.5)
        E("dve", "scalar_tensor_tensor", [SMG, SCT], [SCT], out=scf, in0=small[:, 0:8 * NB], scalar=1.0, in1=scf, op0=ALU.add, op1=ALU.mult)
        E("dve", "tensor_scalar", [SCT], [SCT], out=scf, in0=scf, scalar1=0.5, scalar2=None, op0=ALU.mult)
        mod_ps = pbk[PJ]
        for pi in range(24):
            sl = pi % 2
            wm = xg[sl][:].rearrange("p a b -> p (a b)")
            DMA(sp_q, wm.rearrange("p (k n) -> p k n", k=8), w_mod[:, pi * 256:(pi + 1) * 256].rearrange("(k p) n -> p k n", p=128),
                [], XG[sl], XG[sl][0])
            for c2 in range(2):
                cc = pi * 2 + c2
                for k in range(8):
                    E("pe", "matmul", XG[sl] + [SCT], PBK[PJ], mod_ps[:, cc * NB:(cc + 1) * NB],
                      lhsT=wm[:, k * 256 + c2 * 128: k * 256 + (c2 + 1) * 128], rhs=scT[:, k, :], start=(k == 0), stop=(k == 7))
        E("dve", "tensor_tensor", PBK[PJ] + [VECS], [MODT], out=modT[:], in0=mod_ps[:, 0:48 * NB].rearrange("p (a b) -> p a b", b=NB),
          in1=ap3(vecs, 128, [[1, 48], [0, NB]]), op=ALU.add)
        for n, (c0, g0) in enumerate(((8, 48), (32, 56))):
            E("dve", "scalar_tensor_tensor", [MODT, VECS], [HSC], out=hsc[:, n], in0=modT[:, c0:c0 + 8, :], scalar=1.0,
              in1=ap3(vecs, 128, [[1, 8], [0, NB]], off=g0), op0=ALU.add, op1=ALU.mult)

        def hscale(n, j, b):
            return hsc[:, n, j, b:b + 1]

        def mvec(idx, j, b):
            return modT[:, idx * 8 + j, b:b + 1]

        for k in range(8):
            P.add("pool", lambda e, k=k: e.dma_start(out=win_sb[:, k, :], in_=w_in[k * 128:(k + 1) * 128, WB0:IN_COLS]), writes=[WIN[k]], dma_buf=WIN[k])
        WSCR = {}
        blocks = []
        for i in range(6):
            blocks.append((("a", i), w_in[:, i * 256:(i + 1) * 256].rearrange("(k p) n -> p k n", p=128), "p (k n) -> p k n", 8, wa_s[i]))
        for i in range(4):
            blocks.append((("o", i), w_out[:, i * 256:(i + 1) * 256].rearrange("(k p) n -> p k n", p=128), "p (k n) -> p k n", 8, wo_s[i]))
        for i in range(16):
            blocks.append((("u", i), w_up[:, i * 256:(i + 1) * 256].rearrange("(k p) n -> p k n", p=128), "p (k n) -> p k n", 8, wu_s[i]))
        for dc in range(8):
            for fh in range(2):
                blocks.append((("d", dc * 2 + fh), w_down[fh * 2048:(fh + 1) * 2048, dc * 128:(dc + 1) * 128].rearrange("(f p) n -> p f n", p=128),
                               "p (f n) -> p f n", 16, wd_s[dc * 2 + fh]))
        for bi, (key, src, pat, a0, scr) in enumerate(blocks):
            sl = bi % NWR
            WSCR[key] = Buf("wscr_%s%d" % key)
            dstv = wring[sl][:].rearrange(pat, **({"k": a0} if "k" in pat else {"f": a0}))
            P.add("pool", lambda e, dstv=dstv, src=src: e.dma_start(out=dstv, in_=src), writes=[WRING[sl]], dma_buf=WRING[sl])
            P.add("sp", lambda e, scr=scr, sl=sl: e.dma_start(out=scr, in_=wring[sl][:]), reads=[WRING[sl]], writes=[WSCR[key]], dma_buf=WRING[sl])

        def wload(key, scr):
            ws = gstate.setdefault("wcnt", 0)
            gstate["wcnt"] = ws + 1
            rs = ws % NWR
            DMA("sp", wring[rs][:], scr, [WSCR[key]], [WRING[rs]], WRING[rs])
            return rs

        def norm_to_hT_gen(sl, ti, n, b):
            xt = xg[sl][:, ti, :]
            dT, DTB = (hT, HT) if n == 0 else (h2T, H2T)
            E("act", "activation", [XG[sl][ti]], [XN, SMN], out=xn_bf[:], in_=xt, func=AF.Square, accum_out=smn[:, 0:1])
            yield
            E("pool", "tensor_scalar", [SMN], [SMN], out=smn[:, 1:2], in0=smn[:, 0:1], scalar1=1.0 / D, scalar2=EPS, op0=ALU.mult, op1=ALU.add)
            E("pool", "tensor_tensor", [SMN, MISC], [SMN], out=smn[:, 1:2], in0=smn[:, 1:2], in1=MH, op=ALU.pow)
            yield
            E("act", "activation", [XG[sl][ti], SMN], [XN], out=xn_bf[:], in_=xt, func=AF.Copy, scale=smn[:, 1:2])
            yield
            yield
            trb = pbk[TR].bitcast(BF16)
            for j in range(8):
                E("pe", "transpose", [XN, IDB], PBK[TR], out=trb[:, j * 128:(j + 1) * 128], in_=xn_bf[:, j * 128:(j + 1) * 128], identity=identb[:])
            for j in range(8):
                if (ti + n) % 2 == 0:
                    E("act", "activation", PBK[TR] + [HSC, MODT], [DTB[ti]], out=dT[:, j, ti * 128:(ti + 1) * 128], in_=trb[:, j * 128:(j + 1) * 128],
                      func=AF.Identity, scale=hscale(n, j, b), bias=mvec(3 * n, j, b))
                else:
                    E("dve", "tensor_scalar", PBK[TR] + [HSC, MODT], [DTB[ti]], out=dT[:, j, ti * 128:(ti + 1) * 128], in0=trb[:, j * 128:(j + 1) * 128],
                      scalar1=hscale(n, j, b), scalar2=mvec(3 * n, j, b), op0=ALU.mult, op1=ALU.add)
            yield

        def norm_to_hT(sl, ti, n, b):
            xt = xg[sl][:, ti, :]
            dT, DTB = (hT, HT) if n == 0 else (h2T, H2T)
            E("act", "activation", [XG[sl][ti]], [XN, SMN], out=xn_bf[:], in_=xt, func=AF.Square, accum_out=smn[:, 0:1])
            E("pool", "tensor_scalar", [SMN], [SMN], out=smn[:, 1:2], in0=smn[:, 0:1], scalar1=1.0 / D, scalar2=EPS, op0=ALU.mult, op1=ALU.add)
            E("pool", "tensor_tensor", [SMN, MISC], [SMN], out=smn[:, 1:2], in0=smn[:, 1:2], in1=MH, op=ALU.pow)
            E("act", "activation", [XG[sl][ti], SMN], [XN], out=xn_bf[:], in_=xt, func=AF.Copy, scale=smn[:, 1:2])
            trb = pbk[TR].bitcast(BF16)
            for j in range(8):
                E("pe", "transpose", [XN, IDB], PBK[TR], out=trb[:, j * 128:(j + 1) * 128], in_=xn_bf[:, j * 128:(j + 1) * 128], identity=identb[:])
            for j in range(8):
                eng = "act" if (ti + n) % 2 == 0 else "dve"
                if eng == "act":
                    E("act", "activation", PBK[TR] + [HSC, MODT], [DTB[ti]], out=dT[:, j, ti * 128:(ti + 1) * 128], in_=trb[:, j * 128:(j + 1) * 128],
                      func=AF.Identity, scale=hscale(n, j, b), bias=mvec(3 * n, j, b))
                else:
                    E("dve", "tensor_scalar", PBK[TR] + [HSC, MODT], [DTB[ti]], out=dT[:, j, ti * 128:(ti + 1) * 128], in0=trb[:, j * 128:(j + 1) * 128],
                      scalar1=hscale(n, j, b), scalar2=mvec(3 * n, j, b), op0=ALU.mult, op1=ALU.add)


        tmode = {"fast": False, "n": 0}

        def tbank():
            if tmode["fast"]:
                tmode["n"] += 1
                return PJ if tmode["n"] % 2 == 0 else SC1
            return PJ

        def dense_back(sl, ntl, b, gate_idx, chunk_mm):
            N = 128 * ntl
            for dp in range(4):
                TB = tbank()
                for half in range(2):
                    dc = dp * 2 + half
                    pst = pbk[TB][:, half * 256: half * 256 + N]
                    yield from chunk_mm(dc, pst, PBK[TB])
                for half in range(2):
                    dc = dp * 2 + half
                    pst = pbk[TB][:, half * 256: half * 256 + N]
                    E("act", "activation", PBK[TB] + [MODT], [YT[half]], out=yT_sb[half][:, 0:N], in_=pst, func=AF.Copy, scale=mvec(gate_idx, dc, b))
                for half in range(2):
                    for ti in range(ntl):
                        E("pe", "transpose", [YT[half], CST], PBK[TB], out=pbk[TB][:, ti * 256 + half * 128: ti * 256 + (half + 1) * 128],
                          in_=yT_sb[half][:, ti * 128:(ti + 1) * 128], identity=ident)
                for ti in range(ntl):
                    E("dve", "tensor_tensor", PBK[TB] + [XG[sl][ti]], [XG[sl][ti]], out=xg[sl][:, ti, dp * 256:(dp + 1) * 256],
                      in0=pbk[TB][:, ti * 256:(ti + 1) * 256], in1=xg[sl][:, ti, dp * 256:(dp + 1) * 256], op=ALU.add)
                yield

        pool_banks.remove(PJ)
        pool_banks.remove(SC1)

        def gdn_decay(ti, part, bank=None, gp=0):
            SC = SCA[gp * 2 + ti]

            def bc4(c):
                return ap3(sc2, 128, [[1, 4], [0, 128]], off=(gp * 2 + ti) * 48 + c)
            if part == 0:
                for h in range(4):
                    E("pool", "tensor_scalar", [SC, CST], [GU], out=gU[:, h * 128:(h + 1) * 128], in0=UPi, scalar1=sc2[:, gp * 2 + ti, 12 + h:13 + h], scalar2=None, op0=ALU.mult)
                E("pool", "tensor_tensor", [SC, CST], [MBB], out=MBt.rearrange("p (h e) -> p h e", h=4), in0=ap3(cst, 128, [[0, 4], [1, 128]], off=2 * 128),
                  in1=bc4(36), op=ALU.mult)
                return
            bk = GC if bank is None else bank
            for h in range(4):
                E("pe", "matmul", [GU, CST], PBK[bk], pbk[bk][:, h * 128:(h + 1) * 128], lhsT=gU[:, h * 128:(h + 1) * 128], rhs=LOWs, start=True, stop=True)
            E("act", "activation", PBK[bk], [ESB], out=Esb, in_=pbk[bk][:], func=AF.Exp)
            E("dve", "tensor_tensor", [ESB, CST], [ESB], out=Esb.rearrange("p (h e) -> p h e", h=4), in0=Esb.rearrange("p (h e) -> p h e", h=4),
              in1=ap3(cst, 128, [[0, 4], [1, 128]], off=3 * 128), op=ALU.mult)

        def gdn_tile(b, ti, is_sample, gp=0):
            c0 = ti * 128

            def qh(j, h):
                return qkvT(j * 4 + h, c0, c0 + 128)
            trb = pbk[TR].bitcast(BF16)
            gsil = gsil2[ti]
            GSIL = GSIL2[ti]
            mix_bf = mixb2[ti]
            MIXA = MIXA2[ti]
            SC = SCA[gp * 2 + ti]
            SCK = SCN[gp * 2 + ti]

            def bc4(c):
                return ap3(sc2, 128, [[1, 4], [0, 128]], off=(gp * 2 + ti) * 48 + c)

            def bcs(c):
                return ap3(small, 128, [[1, 4], [0, 128]], off=c)
            for h in range(4):
                E("pe", "transpose", [QKB[4 + h], IDB], PBK[TR], out=trb[:, h * 128:(h + 1) * 128], in_=qh(1, h), identity=identb[:])
            for h in range(4):
                E("pe", "transpose", [QKB[8 + h], IDB], PBK[TR], out=trb[:, 512 + h * 128: 512 + (h + 1) * 128], in_=qh(2, h), identity=identb[:])
            kt3 = trb[:, 0:512].rearrange("p (h e) -> p h e", h=4)
            vt3 = trb[:, 512:1024].rearrange("p (h e) -> p h e", h=4)
            for i, c in enumerate((4, 28, 32)):
                E("dve", "tensor_tensor", PBK[TR] + [SCK], [KN3], out=kn3[:, i, :].rearrange("p (h e) -> p h e", h=4), in0=kt3, in1=bc4(c), op=ALU.mult)
            E("dve", "tensor_tensor", PBK[TR] + [SC], [VBT], out=vbt[:].rearrange("p (h e) -> p h e", h=4), in0=vt3, in1=bc4(8), op=ALU.mult)
            yield
            for h in range(4):
                E("pe", "transpose", [KN3, IDB], PBK[TR], out=trb[:, h * 128:(h + 1) * 128], in_=kn3[:, 0, h * 128:(h + 1) * 128], identity=identb[:])
            E("act", "copy", PBK[TR], [KNT], out=knT[:], in_=trb[:, 0:512])
            yield
            for h in range(4):
                E("pe", "matmul", [KNT], PBK[GA], pbk[GA][:, h * 128:(h + 1) * 128], lhsT=knT[:, h * 128:(h + 1) * 128], rhs=knT[:, h * 128:(h + 1) * 128], start=True, stop=True)
            for h in range(4):
                E("pe", "matmul", [KNT, QKB[h]], PBK[GB], pbk[GB][:, h * 128:(h + 1) * 128], lhsT=qh(0, h), rhs=knT[:, h * 128:(h + 1) * 128], start=True, stop=True)
            E("dve", "tensor_tensor", PBK[GA] + [ESB], [T1], out=t1[:], in0=pbk[GA][:], in1=Esb, op=ALU.mult)
            E("dve", "tensor_tensor", [T1, MBB], [NTB[0]], out=NTb[0][:], in0=t1[:], in1=MBt, op=ALU.mult)
            E("dve", "tensor_tensor", PBK[GB] + [ESB], [QKBF], out=qkb[:], in0=pbk[GB][:], in1=Esb, op=ALU.mult)
            yield
            def gm(i):
                return ap3(gmb, 128, [[0, 4], [1, 128]], off=i * 128)

            def v4(t):
                return t[:].rearrange("p (h e) -> p h e", h=4)
            nT0, nbT, Xs = NTb[0], NTb[1], NQ[1]
            NT0B, NBTB, XSB = NTB[0], NTB[1], NQB[1]
            E("pool", "tensor_tensor", [NT0B, GMB], [NBTB], out=v4(nbT), in0=v4(nT0), in1=gm(0), op=ALU.mult)
            for h in range(4):
                E("pe", "transpose", [NBTB, IDB], PBK[TR], out=trb[:, h * 128:(h + 1) * 128], in_=nbT[:, h * 128:(h + 1) * 128], identity=identb[:])
            for h in range(4):
                E("pe", "transpose", [QKBF, IDB], PBK[TR], out=trb[:, 512 + h * 128:512 + (h + 1) * 128], in_=qkb[:, h * 128:(h + 1) * 128], identity=identb[:])
            E("act", "copy", PBK[TR], [NQB[0]], out=NQ[0][:], in_=trb[:, 0:1024])
            E("dve", "tensor_tensor", PBK[TR] + [IDB], [YB[0]], out=Yb[0][:].rearrange("p (h e) -> p h e", h=4), in0=trb[:, 0:512].rearrange("p (h e) -> p h e", h=4),
              in1=ap3(identb, 128, [[0, 4], [1, 128]]), op=ALU.add)
            nb = NQ[0][:, 0:512]
            yield

            def mm4(bank, lhs, rhs, rl, rr):
                for h in range(4):
                    E("pe", "matmul", rl + rr, PBK[bank], pbk[bank][:, h * 128:(h + 1) * 128], lhsT=lhs[:, h * 128:(h + 1) * 128], rhs=rhs[:, h * 128:(h + 1) * 128], start=True, stop=True)
            mm4(GA, nbT, nb, [NBTB], [NQB[0]])
            mm4(GB, nb, nbT, [NQB[0]], [NBTB])
            E("act", "copy", PBK[GA], [GBF[0]], out=gbf[0][:], in_=pbk[GA][:])
            E("dve", "tensor_copy", PBK[GB], [GBF[1]], out=gbf[1][:], in_=pbk[GB][:])
            yield
            mm4(GC, gbf[1], Yb[0], [GBF[1]], [YB[0]])
            mm4(GA, gbf[0], gbf[1], [GBF[0]], [GBF[1]])
            E("dve", "tensor_tensor", PBK[GC] + [YB[0]], [YB[1]], out=Yb[1][:], in0=pbk[GC][:], in1=Yb[0][:], op=ALU.add)
            E("act", "copy", PBK[GA], [GBF[2]], out=gbf[2][:], in_=pbk[GA][:])
            yield
            mm4(GC, gbf[2], Yb[1], [GBF[2]], [YB[1]])
            E("dve", "tensor_tensor", PBK[GC] + [YB[1]], [YB[0]], out=Yb[0][:], in0=pbk[GC][:], in1=Yb[1][:], op=ALU.add)
            yield
            cur = 0
            E("pool", "tensor_tensor", [NT0B, GMB], [GBF[3]], out=v4(gbf[3]), in0=v4(nT0), in1=gm(1), op=ALU.mult)
            for li in range(4):
                U, UB = Yb[cur], YB[cur]
                LM, LMB = gbf[3 + li % 2], GBF[3 + li % 2]
                mm4(GA, LM, U, [LMB], [UB])
                if li < 3:
                    E("pool", "tensor_tensor", [NT0B, GMB], [GBF[3 + (li + 1) % 2]], out=v4(gbf[3 + (li + 1) % 2]), in0=v4(nT0), in1=gm(2 + li), op=ALU.mult)
                for h in range(4):
                    E("pe", "transpose", [UB, IDB], PBK[TR], out=trb[:, h * 128:(h + 1) * 128], in_=U[:, h * 128:(h + 1) * 128], identity=identb[:])
                E("act", "copy", PBK[GA], [GBF[0]], out=gbf[0][:], in_=pbk[GA][:])
                E("dve", "tensor_copy", PBK[TR], [XSB], out=Xs[:], in_=trb[:, 0:512])
                yield
                mm4(GC, Xs, gbf[0], [XSB], [GBF[0]])
                E("dve", "tensor_tensor", PBK[GC] + [UB], [YB[1 - cur]], out=Yb[1 - cur][:], in0=pbk[GC][:], in1=U[:], op=ALU.add)
                cur = 1 - cur
                yield
            XT = Yb[cur]
            XTB = YB[cur]
            qkT = NQ[0][:, 512:1024]
            yield
            for h in range(4):
                E("pe", "matmul", [KN3, XTB], PBK[GA], pbk[GA][:, h * 128:(h + 1) * 128], lhsT=kn3[:, 1, h * 128:(h + 1) * 128], rhs=XT[:, h * 128:(h + 1) * 128], start=True, stop=True)
            E("act", "activation", PBK[GA], [NWT], out=nwT[:], in_=pbk[GA][:], func=AF.Copy, scale=-1.0)
            for h in range(4):
                E("pe", "matmul", [QKB[h], SBF], PBK[GC], pbk[GC][:, h * 128:(h + 1) * 128], lhsT=qh(0, h), rhs=Sbf[:, h * 128:(h + 1) * 128], start=True, stop=True)
            yield
            for h in range(4):
                E("pe", "matmul", [XTB, VBT], PBK[GB], pbk[GB][:, h * 128:(h + 1) * 128], lhsT=XT[:, h * 128:(h + 1) * 128], rhs=vbt[:, h * 128:(h + 1) * 128], start=True, stop=False)
                E("pe", "matmul", [NWT, SBF], PBK[GB], pbk[GB][:, h * 128:(h + 1) * 128], lhsT=nwT[:, h * 128:(h + 1) * 128], rhs=Sbf[:, h * 128:(h + 1) * 128], start=False, stop=True)
            E("act", "copy", PBK[GB], [VNEW], out=vnew[:], in_=pbk[GB][:])
            E("dve", "tensor_tensor", PBK[GC] + [SC], [OT], out=ot[:].rearrange("p (h e) -> p h e", h=4), in0=pbk[GC][:].rearrange("p (h e) -> p h e", h=4), in1=bc4(16), op=ALU.mult)
            yield
            for h in range(4):
                E("pe", "matmul", [NQB[0], VNEW], PBK[GA], pbk[GA][:, h * 128:(h + 1) * 128], lhsT=qkT[:, h * 128:(h + 1) * 128], rhs=vnew[:, h * 128:(h + 1) * 128], start=True, stop=True)
            for h in range(4):
                E("pe", "matmul", [KN3, VNEW], PBK[GB], pbk[GB][:, h * 128:(h + 1) * 128], lhsT=kn3[:, 2, h * 128:(h + 1) * 128], rhs=vnew[:, h * 128:(h + 1) * 128], start=True, stop=True)
            E("dve", "tensor_tensor", PBK[GA] + [OT], [OT], out=ot[:], in0=pbk[GA][:], in1=ot[:], op=ALU.add)
            E("dve", "tensor_tensor", [SB32, SC], [SB32], out=S32[:].rearrange("p (h e) -> p h e", h=4), in0=S32[:].rearrange("p (h e) -> p h e", h=4), in1=bc4(24), op=ALU.mult)
            E("dve", "tensor_tensor", PBK[GB] + [SB32], [SB32], out=S32[:], in0=pbk[GB][:], in1=S32[:], op=ALU.add)
            E("act", "copy", [SB32], [SBF], out=Sbf[:], in_=S32[:])
            rq = sc2[:, gp * 2 + ti, 0:4]
            E("act", "activation", [OT], [T1], out=t1[:], in_=ot[:], func=AF.Square)
            E("dve", "tensor_reduce", [T1], [SMG], out=small[:, 44:48], in_=t1[:].rearrange("p (h e) -> p h e", h=4), axis=AX.X, op=ALU.add)
            E("dve", "tensor_tensor", [SCK], [SMG], out=small[:, 60:64], in0=rq, in1=rq, op=ALU.mult)
            E("dve", "tensor_tensor", [SMG], [SMG], out=small[:, 60:64], in0=small[:, 60:64], in1=small[:, 44:48], op=ALU.mult)
            E("dve", "tensor_scalar", [SMG], [SMG], out=small[:, 60:64], in0=small[:, 60:64], scalar1=1.0 / 128, scalar2=EPS, op0=ALU.mult, op1=ALU.add)
            E("pool", "tensor_tensor", [SMG, MISC], [SMG], out=small[:, 60:64], in0=small[:, 60:64], in1=ap3(misc, 128, [[0, 4]]), op=ALU.pow)
            E("dve", "tensor_tensor", [SMG, SCK], [SMG], out=small[:, 48:52], in0=small[:, 60:64], in1=rq, op=ALU.mult)
            E("dve", "tensor_tensor", [OT, SMG], [OT], out=ot[:].rearrange("p (h e) -> p h e", h=4), in0=ot[:].rearrange("p (h e) -> p h e", h=4), in1=bcs(48), op=ALU.mult)
            E("dve", "tensor_tensor", [OT, GNG], [OT], out=ot[:].rearrange("p (h e) -> p h e", h=4), in0=ot[:].rearrange("p (h e) -> p h e", h=4), in1=ap3(gng4, 128, [[0, 4], [1, 128]]), op=ALU.mult)
            E("dve", "tensor_tensor", [OT, GSIL], [MIXA], out=mix_bf[:, 0:512], in0=ot[:], in1=gsil[:], op=ALU.mult)
            yield

        def attn_tile(b, ti, tglob, slot_of):
            trb = pbk[TR].bitcast(BF16)
            mix_bf = mixb2[ti]
            MIXB = MIXB2[ti]
            for i in range(2):
                E("act", "activation", [BSB[i]], [SCB[i]], out=scb[i][:], in_=bsb[:, i * 512:(i + 1) * 512], func=AF.Square)
                E("dve", "tensor_reduce", [SCB[i]], [SMA], out=small[:, 64 + i * 8:72 + i * 8], in_=scb[i][:].rearrange("p (h e) -> p h e", h=8), axis=AX.X, op=ALU.add)
            E("dve", "tensor_scalar", [SMA], [SMA], out=small[:, 64:80], in0=small[:, 64:80], scalar1=1.0 / 64, scalar2=EPS, op0=ALU.mult, op1=ALU.add)
            E("pool", "tensor_tensor", [SMA, MISC], [SMA], out=small[:, 64:80], in0=small[:, 64:80], in1=ap3(misc, 128, [[0, 16]]), op=ALU.pow)
            for i in range(2):
                v8 = bsb[:, i * 512:(i + 1) * 512].rearrange("p (h e) -> p h e", h=8)
                E("dve", "tensor_tensor", [BSB[i], SMA], [BSB[i]], out=v8, in0=v8, in1=ap3(small, 128, [[1, 8], [0, 64]], off=64 + i * 8), op=ALU.mult)
                E("dve", "tensor_tensor", [BSB[i], QKG], [BSB[i]], out=v8, in0=v8, in1=ap3(qkg8, 128, [[0, 8], [1, 64]], off=64 * i), op=ALU.mult)
            yield
            E("act", "copy", [BSB[0], BSB[1]], [QKN], out=qkn_bf[:], in_=bsb[:, 0:1024])
            sl = slot_of(tglob)
            if b >= n_pseq:
                E("pool", "memset", [], [VRING[sl]], vring[:, sl], 0.0)
                E("pool", "tensor_copy", [BSB[2]], [VRING[sl]], out=vring[0:16, sl, :, 0:64], in_=bsb[0:16, 1024:1536].rearrange("p (h e) -> p h e", h=8))
                E("pool", "memset", [], [VRING[sl]], vring[0:16, sl, :, 64:65], 1.0)
            else:
                E("pool", "tensor_copy", [BSB[2]], [VRING[sl]], out=vring[:, sl, :, 0:64], in_=bsb[:, 1024:1536].rearrange("p (h e) -> p h e", h=8))
            for j in range(8):
                E("pe", "transpose", [QKN, IDB], PBK[TR], out=trb[:, j * 128:(j + 1) * 128], in_=qkn_bf[:, j * 128:(j + 1) * 128], identity=identb[:])
            E("act", "activation", PBK[TR], [QT], out=qTz[0][0:64].rearrange("p a b -> p (a b)"), in_=trb[0:64, 0:512], func=AF.Copy, scale=0.125)
            E("act", "activation", PBK[TR], [QT], out=qTz[1][64:128].rearrange("p a b -> p (a b)"), in_=trb[64:128, 0:512], func=AF.Copy, scale=0.125)
            E("dve", "tensor_copy", PBK[TR], [KRING[sl]], out=kring[:, :, sl * 128:(sl + 1) * 128], in_=trb[:, 512:1024].rearrange("p (a b) -> p a b", a=4))
            yield
            dts = [dt for dt in range(5) if tglob - dt >= 0]
            u = 0
            for hh in range(2):
                base = 64 * hh
                for dt in dts:
                    ks = slot_of(tglob - dt)
                    bank = SC0
                    for i in range(4):
                        E("pe", "matmul", [KRING[ks], QT], PBK[bank], pbk[bank][:, i * 128:(i + 1) * 128],
                          lhsT=kring[:, i, ks * 128:(ks + 1) * 128], rhs=qTz[hh][:, i, :], start=(i == 0), stop=False)
                    E("pe", "matmul", [BIAS, IDB], PBK[bank], pbk[bank][:].rearrange("p (h e) -> p h e", h=4), lhsT=identb[:],
                      rhs=ap3(bias_sb, 128, [[256, 4], [1, 128]], off=dt * 1024 + hh * 128), start=False, stop=True)
                    E("act", "activation", PBK[bank], [PB[hh][dt]], out=Pb[hh][:, dt, :], in_=pbk[bank][:], func=AF.Exp)
                    u += 1
                    if u % 2 == 0:
                        yield
                pvv = pbk[PV][:].rearrange("p (h e) -> p h e", h=4)
                for i in range(4):
                    h = 2 * i + hh
                    for n_, dt in enumerate(dts):
                        ks = slot_of(tglob - dt)
                        E("pe", "matmul", [PB[hh][dt], VRING[ks]], PBK[PV], pvv[:, i, 0:65], lhsT=Pb[hh][:, dt, i * 128:(i + 1) * 128],
                          rhs=vring[:, ks, h, :], start=(n_ == 0), stop=(n_ == len(dts) - 1))
                E("dve", "reciprocal", PBK[PV], [SMA], out=small[:, 80 + hh * 4: 84 + hh * 4].rearrange("p (h e) -> p h e", e=1), in_=pvv[:, :, 64:65])
                E("dve", "tensor_tensor", PBK[PV] + [SMA], [MIXB[hh]], out=ap3(mix_bf, 128, [[128, 4], [1, 64]], off=512 + hh * 64),
                  in0=pvv[:, :, 0:64], in1=ap3(small, 128, [[1, 4], [0, 64]], off=80 + hh * 4), op=ALU.mult)
                yield

        onesb = sb("onesb", [128, 2], BF16); ONB = Buf("onesb")
        E("pool", "memset", [], [ONB], onesb[:], 1.0)
        CST_ONB = ONB

        import os
        STOPF = float(os.environ.get("DBG_STOP", "99"))
        gstate = {"g": 0}
        pcv_all = R1[:, 0:12 * 259].rearrange("p (j n) -> p j n", j=12)

        def load_x(g, b, t0, ntl, is_sample, sidx):
            sl = g % 2
            for ti in range(ntl):
                if is_sample:
                    E("pool", "memset", [], [XG[sl][ti]], xg[sl][:, ti, :], 0.0)
                    DMA("sp", xg[sl][0:16, ti, :], xs[sidx], [], [XG[sl][ti]], XG[sl][ti])
                else:
                    DMA("sp", xg[sl][:, ti, :], xp[b, (t0 + ti) * 128:(t0 + ti + 1) * 128, :], [], [XG[sl][ti]], XG[sl][ti])

        ginfo = {}
        early_norm = {}

        def run_group_early(g, b, t0, ntl, is_sample, sidx, nkf, getbank, fresh_ctx):
            sl = g % 2
            gp = g % 2
            N = 128 * ntl
            for ti in range(ntl):
                yield from norm_to_hT_gen(sl, ti, 0, b)
            HTg = HT[0:ntl]
            SCg = SCA[gp * 2: gp * 2 + ntl]
            SCBg = SCN[gp * 2: gp * 2 + ntl]
            bank = getbank()
            for ti in range(ntl):
                for k in range(8):
                    E("pe", "matmul", [HT[ti], WIN[k]], PBK[bank], pbk[bank][:, ti * 8:(ti + 1) * 8], lhsT=hT[:, k, ti * 128:(ti + 1) * 128], rhs=win_sb[:, k, BA_ - WB0:QB_ - WB0],
                      start=(k == 0), stop=(k == 7))
            yield
            yield
            E("dve", "tensor_copy", PBK[bank], SCg, out=sc2[:, gp * 2: gp * 2 + ntl, 40:48], in_=pbk[bank][:, 0:ntl * 8].rearrange("p (t c) -> p t c", t=ntl))
            E("act", "activation", SCg, SCg, out=sc2[:, gp * 2: gp * 2 + ntl, 8:12], in_=sc2[:, gp * 2: gp * 2 + ntl, 40:44], func=AF.Tanh, scale=0.5)
            E("dve", "tensor_scalar", SCg, SCg, out=sc2[:, gp * 2: gp * 2 + ntl, 8:12], in0=sc2[:, gp * 2: gp * 2 + ntl, 8:12], scalar1=0.5, scalar2=0.5, op0=ALU.mult, op1=ALU.add)
            E("dve", "tensor_tensor", SCg + [MISC], SCg, out=sc2[:, gp * 2: gp * 2 + ntl, 12:16], in0=sc2[:, gp * 2: gp * 2 + ntl, 44:48], in1=ap3(misc, 128, [[0, ntl], [1, 4]], off=4), op=ALU.add)
            E("act", "activation", SCg, SCg, out=sc2[:, gp * 2: gp * 2 + ntl, 12:16], in_=sc2[:, gp * 2: gp * 2 + ntl, 12:16], func=AF.Exp)
            E("act", "activation", SCg, SCg, out=sc2[:, gp * 2: gp * 2 + ntl, 12:16], in_=sc2[:, gp * 2: gp * 2 + ntl, 12:16], func=AF.Ln, bias=1.0)
            E("dve", "tensor_tensor", SCg + [MISC], SCg, out=sc2[:, gp * 2: gp * 2 + ntl, 12:16], in0=sc2[:, gp * 2: gp * 2 + ntl, 12:16], in1=ap3(misc, 128, [[0, ntl], [1, 4]], off=8), op=ALU.mult)
            if is_sample:
                E("dve", "tensor_scalar", SCg + [MISC], SCg, out=sc2[:, gp * 2: gp * 2 + ntl, 8:16], in0=sc2[:, gp * 2: gp * 2 + ntl, 8:16], scalar1=VAL, scalar2=None, op0=ALU.mult)
            E("dve", "tensor_scalar", SCg, SCg, out=sc2[:, gp * 2: gp * 2 + ntl, 36:40], in0=sc2[:, gp * 2: gp * 2 + ntl, 8:12], scalar1=-1.0, scalar2=None, op0=ALU.mult)
            yield
            yield
            bank = getbank()
            for ti in range(ntl):
                for i, lh in enumerate((UPi, LOWs, ones)):
                    E("pe", "matmul", [SCA[gp * 2 + ti], CST], PBK[bank], pbk[bank][:, ti * 12 + i * 4: ti * 12 + (i + 1) * 4], lhsT=lh, rhs=sc2[:, gp * 2 + ti, 12:16], start=True, stop=True)
            E("act", "activation", PBK[bank], SCg, out=sc2[:, gp * 2: gp * 2 + ntl, 16:28], in_=pbk[bank][:, 0:ntl * 12].rearrange("p (t c) -> p t c", t=ntl), func=AF.Exp)
            yield
            if fresh_ctx:
                E("pool", "memset", [], PCB, pcv_all[:, :, 0:3], 0.0)
            else:
                E("pool", "tensor_copy", [CCTX], PCB, out=pcv_all[:, :, 0:3], in_=cctx[:])
            for jp in range(6):
                bank = getbank()
                rs = wload(("a", jp), wa_s[jp])
                wv = wring[rs][:].rearrange("p (k n) -> p k n", k=8)
                for half in range(2):
                    j = jp * 2 + half
                    pst = pbk[bank][:, half * 256: half * 256 + N]
                    for k in range(8):
                        E("pe", "matmul", HTg + [WRING[rs]], PBK[bank], pst, lhsT=wv[:, k, half * 128:(half + 1) * 128], rhs=hT[:, k, 0:N], start=(k == 0), stop=(k == 7))
                yield
                for half in range(2):
                    j = jp * 2 + half
                    pst = pbk[bank][:, half * 256: half * 256 + N]
                    E("act", "copy", PBK[bank], [PCB[j]], out=pc(j, 3, 3 + N), in_=pst)
                yield
            ginfo[g] = (sl, N)

        def run_group_rest(g, b, t0, ntl, is_sample, sidx, nkf, last):
            sl, N = ginfo[g]
            gp = g % 2
            HTg = HT[0:ntl]
            SCg = SCA[gp * 2: gp * 2 + ntl]
            SCBg = SCN[gp * 2: gp * 2 + ntl]
            gdn_decay(0, 0, gp=gp)
            yield
            gdn_decay(0, 1, gp=gp)
            yield
            mixer_inputs = []
            for ti in range(ntl):
                def bproj(c0, c1, ti=ti, banks=None):
                    bank = next_full() if banks is None else banks.pop(0)
                    pst = pbk[bank][:, 0:c1 - c0]
                    for k in range(8):
                        E("pe", "matmul", [HT[ti], WIN[k]], PBK[bank], pst, lhsT=hT[:, k, ti * 128:(ti + 1) * 128], rhs=win_sb[:, k, c0 - WB0:c1 - WB0], start=(k == 0), stop=(k == 7))
                    return bank, pst
                mixer_inputs.append(bproj)
            ginfo[g] = (sl, N, mixer_inputs)
            tile_front(b, 0, t0, mixer_inputs[0], is_sample, sidx, nkf)
            yield
            pend = None

            def conv_final(j):
                a_, t_ = cacc[j % 2], ctnh[j % 2]
                A_, T_ = CACC[j % 2], CTNH[j % 2]
                E("dve", "scalar_tensor_tensor", [A_, T_], [QKB[j]], out=qkvT(j, 0, N), in0=t_[:, 0:N], scalar=1.0, in1=a_[:, 0:N], op0=ALU.add, op1=ALU.mult)
                if j < 8:
                    E("act", "activation", [QKB[j]], [SQT], out=sqT[:, j, 0:N], in_=qkvT(j, 0, N), func=AF.Square)
            for j in range(12):
                a_, t_ = cacc[j % 2], ctnh[j % 2]
                A_, T_ = CACC[j % 2], CTNH[j % 2]
                E("act", "activation", [PCB[j], CW], [A_], out=a_[:, 0:N], in_=pc(j, 0, N), func=AF.Copy, scale=cw[:, j, 0:1])
                for i in range(1, 4):
                    E("dve", "scalar_tensor_tensor", [PCB[j], CW, A_], [A_], out=a_[:, 0:N], in0=pc(j, i, i + N), scalar=cw[:, j, i:i + 1], in1=a_[:, 0:N], op0=ALU.mult, op1=ALU.add)
                E("act", "activation", [A_], [T_], out=t_[:, 0:N], in_=a_[:, 0:N], func=AF.Tanh)
                if pend is not None:
                    conv_final(pend)
                pend = j
                yield
            conv_final(pend)
            bank = next_full()
            for ti in range(ntl):
                for j in range(8):
                    E("pe", "matmul", [SQT, ONB], PBK[bank], pbk[bank][:, ti * 8 + j: ti * 8 + j + 1], lhsT=sqT[:, j, ti * 128:(ti + 1) * 128], rhs=onesb[:, 0:1], start=True, stop=True)
            E("dve", "tensor_scalar", PBK[bank], SCBg, out=sc2[:, gp * 2: gp * 2 + ntl, 0:8], in0=pbk[bank][:, 0:ntl * 8].rearrange("p (t c) -> p t c", t=ntl), scalar1=EPS, scalar2=None, op0=ALU.add)
            E("pool", "tensor_tensor", SCBg + [MISC], SCBg, out=sc2[:, gp * 2: gp * 2 + ntl, 0:8], in0=sc2[:, gp * 2: gp * 2 + ntl, 0:8], in1=ap3(misc, 128, [[0, ntl], [0, 8]]), op=ALU.pow)
            E("dve", "tensor_scalar", SCBg, SCBg, out=sc2[:, gp * 2: gp * 2 + ntl, 0:4], in0=sc2[:, gp * 2: gp * 2 + ntl, 0:4], scalar1=128.0 ** -0.5, scalar2=None, op0=ALU.mult)
            E("dve", "tensor_tensor", SCg + SCBg, SCBg, out=sc2[:, gp * 2: gp * 2 + ntl, 28:32], in0=sc2[:, gp * 2: gp * 2 + ntl, 4:8], in1=sc2[:, gp * 2: gp * 2 + ntl, 8:12], op=ALU.mult)
            E("dve", "tensor_tensor", SCg + SCBg, SCBg, out=sc2[:, gp * 2: gp * 2 + ntl, 28:32], in0=sc2[:, gp * 2: gp * 2 + ntl, 28:32], in1=sc2[:, gp * 2: gp * 2 + ntl, 16:20], op=ALU.mult)
            E("dve", "tensor_tensor", SCg + SCBg, SCBg, out=sc2[:, gp * 2: gp * 2 + ntl, 32:36], in0=sc2[:, gp * 2: gp * 2 + ntl, 4:8], in1=sc2[:, gp * 2: gp * 2 + ntl, 20:24], op=ALU.mult)
            finish_conv_ctx(b, N, last, is_sample, sidx)

        def finish_conv_ctx(b, N, last, is_sample, sidx):
            pcv = R1[:, 0:12 * 259].rearrange("p (j n) -> p j n", j=12)
            if last:
                c0 = 16 if is_sample else N
                dsrc = (conv_s[sidx] if is_sample else conv_p[b])
                for r in range(3):
                    for j in range(12):
                        P.stores.append(P.add("sp", lambda e, r=r, j=j: e.dma_start(out=dsrc[r, j * 128:(j + 1) * 128].rearrange("(p o) -> p o", o=1),
                                                                                   in_=pcv[:, j, c0 + r:c0 + r + 1], allow_slow_non_contiguous=True),
                                              reads=[PCB[j]], dma_buf=PCB[j]))
            else:
                E("pool", "tensor_copy", PCB, [CCTX], out=cctx[:], in_=pcv[:, :, N:N + 3])

        def tile_front(b, ti, tseq, bproj, is_sample, sidx, n_keep_from, banks=None):
            for i, (c0, c1) in enumerate(((QB_, KB_), (KB_, VB_), (VB_, IN_COLS))):
                bank, pst = bproj(c0, c1, banks=banks)
                E("act" if i != 1 else "dve", "copy" if i != 1 else "tensor_copy", PBK[bank], [BSB[i]], out=bsb[:, i * 512:(i + 1) * 512], in_=pst)
            bank, pst = bproj(GA_, BA_, banks=banks)
            gsil, GSIL = gsil2[ti], GSIL2[ti]
            E("act", "activation", PBK[bank], [GSIL], out=gsil[:], in_=pst, func=AF.Tanh, scale=0.5)
            E("dve", "scalar_tensor_tensor", PBK[bank] + [GSIL], [GSIL], out=gsil[:], in0=gsil[:], scalar=1.0, in1=pst, op0=ALU.add, op1=ALU.mult)
            if is_sample:
                P.stores.append(P.add("sp", lambda e: e.dma_start(out=vn_s[sidx], in_=bsb[0:16, 1024:1536]), reads=[BSB[2]], dma_buf=BSB[2]))
            elif tseq >= n_keep_from:
                r0 = (tseq - n_keep_from) * 128
                P.stores.append(P.add("sp", lambda e: e.dma_start(out=vb_p[b, r0:r0 + 128, :], in_=bsb[:, 1024:1536]), reads=[BSB[2]], dma_buf=BSB[2]))

        def tile_body(b, ti, tseq, tglob, slot_of, is_sample, sidx, n_keep_from, hook=None, gp=0):
            g1 = gdn_tile(b, ti, is_sample, gp)
            g2 = attn_tile(b, ti, tglob, slot_of)
            next(g1)
            next(g2)
            if is_sample:
                P.stores.append(P.add("sp", lambda e: e.dma_start(out=kn_s[sidx], in_=bsb[0:16, 512:1024]), reads=[BSB[1]], dma_buf=BSB[1]))
            elif tseq >= n_keep_from:
                r0 = (tseq - n_keep_from) * 128
                P.stores.append(P.add("sp", lambda e: e.dma_start(out=kb_p[b, r0:r0 + 128, :], in_=bsb[:, 512:1024]), reads=[BSB[1]], dma_buf=BSB[1]))
            alive = [g1, g2]
            rnd = 0
            while alive:
                rnd += 1
                if hook is not None and rnd == hook[0]:
                    hook[1]()
                    hook = None
                for gen in list(alive):
                    if gen is g2 and rnd <= 4 and g1 in alive:
                        continue
                    try:
                        next(gen)
                    except StopIteration:
                        alive.remove(gen)
                yield
            if hook is not None:
                hook[1]()

        def tile_back(ti):
            trb = pbk[TR].bitcast(BF16)
            mix_bf = mixb2[ti]
            for j in range(8):
                E("pe", "transpose", [MIXA2[ti], MIXB2[ti][0], MIXB2[ti][1], IDB], PBK[TR], out=trb[:, j * 128:(j + 1) * 128], in_=mix_bf[:, j * 128:(j + 1) * 128], identity=identb[:])
            E("act", "copy", PBK[TR], [MIXT[ti]], out=mixT[:, :, ti * 128:(ti + 1) * 128], in_=trb[:, 0:1024].rearrange("p (a b) -> p a b", a=8))

        def group_tail(b, sl, ntl, t0, is_sample, sidx):
            N = 128 * ntl
            wst = {}

            def mm_out(dc, pst, wr):
                if dc % 2 == 0:
                    wst["rs"] = wload(("o", dc // 2), wo_s[dc // 2])
                rs = wst["rs"]
                wv = wring[rs][:].rearrange("p (k n) -> p k n", k=8)
                for k in range(8):
                    E("pe", "matmul", MIXT[0:ntl] + [WRING[rs]], wr, pst, lhsT=wv[:, k, (dc % 2) * 128:(dc % 2 + 1) * 128], rhs=mixT[:, k, 0:N], start=(k == 0), stop=(k == 7))
                return
                yield
            yield from dense_back(sl, ntl, b, 2, mm_out)
            for ti in range(ntl):
                norm_to_hT(sl, ti, 1, b)
                yield
            for blk in range(16):
                rs = wload(("u", blk), wu_s[blk])
                wv = wring[rs][:].rearrange("p (k n) -> p k n", k=8)
                bank = tbank()
                for s2 in range(2):
                    pst = pbk[bank][:, s2 * 256: s2 * 256 + N]
                    for k in range(8):
                        E("pe", "matmul", H2T[0:ntl] + [WRING[rs]], PBK[bank], pst, lhsT=wv[:, k, s2 * 128:(s2 + 1) * 128], rhs=h2T[:, k, 0:N], start=(k == 0), stop=(k == 7))
                for s2 in range(2):
                    pst = pbk[bank][:, s2 * 256: s2 * 256 + N]
                    f = blk * 2 + s2
                    fs = f % 2
                    E("act", "activation", PBK[bank], [FFT[fs]], out=fft[fs][:, 0:N], in_=pst, func=AF.Relu)
                    E("dve" if s2 == 0 else "pool", "tensor_tensor", [FFT[fs]], [FFB], out=ffT(f, 0, N), in0=fft[fs][:, 0:N], in1=fft[fs][:, 0:N], op=ALU.mult)
                yield

            def mm_down(dc, pst, wr):
                for fh in range(2):
                    rs = wload(("d", dc * 2 + fh), wd_s[dc * 2 + fh])
                    wv = wring[rs][:].rearrange("p (f n) -> p f n", f=16)
                    for f16 in range(16):
                        f = fh * 16 + f16
                        E("pe", "matmul", [FFB, WRING[rs]], wr, pst, lhsT=wv[:, f16, :], rhs=ffT(f, 0, N), start=(f == 0), stop=(f == 31))
                    if fh == 0:
                        yield
            yield from dense_back(sl, ntl, b, 5, mm_down)
            for ti in range(ntl):
                if is_sample:
                    P.stores.append(P.add("sp", lambda e, ti=ti: e.dma_start(out=y_s[sidx], in_=xg[sl][0:16, ti, :]), reads=[XG[sl][ti]], dma_buf=XG[sl][ti]))
                else:
                    r0 = (t0 + ti) * 128
                    P.stores.append(P.add("sp", lambda e, ti=ti, r0=r0: e.dma_start(out=y_p[b, r0:r0 + 128, :], in_=xg[sl][:, ti, :]), reads=[XG[sl][ti]], dma_buf=XG[sl][ti]))

        groups = []
        ntl_p = 2 if NT % 2 == 0 else 1
        for b in range(n_pseq):
            for t0 in range(0, NT, ntl_p):
                groups.append(dict(b=b, t0=t0, ntl=ntl_p, sample=False, sidx=0, first=(t0 == 0), last=(t0 + ntl_p >= NT)))
        for s_ in range(n_sseq):
            groups.append(dict(b=n_pseq + s_, t0=0, ntl=1, sample=True, sidx=s_, first=True, last=True))
        def state_init(g):
            G = groups[g]
            is_sample, sidx = G["sample"], G["sidx"]
            if G["first"] and not is_sample:
                E("pool", "memset", [], [CCTX], cctx[:], 0.0)
                E("pool", "memset", [], [SB32], S32[:], 0.0)
                E("pool", "memset", [], [SBF], Sbf[:], 0.0)
            if is_sample:
                s_ = sidx
                for r in range(3):
                    for j in range(12):
                        DMA("sp", cctx[:, j, r:r + 1], sconv[s_][r, j * 128:(j + 1) * 128].rearrange("(p o) -> p o", o=1), [], [CCTX], CCTX, allow_slow_non_contiguous=True)
                DMA("sp", S32[:].rearrange("p (h e) -> p h e", h=4), sgdn[s_].rearrange("h d e -> d h e"), [], [SB32], SB32)
                E("act", "copy", [SB32], [SBF], out=Sbf[:], in_=S32[:])
                so = 1 - (g % 2)
                stg = xg[so][:].rearrange("p a b -> p (a b)")
                trb = pbk[TR].bitcast(BF16)
                DMA("sp", stg.rearrange("p (t n) -> p t n", t=4), ckd[s_].rearrange("(t p) n -> p t n", p=128), [], XG[so], XG[so][0])
                for t in range(4):
                    E("pool", "tensor_copy", XG[so], [QKN], out=qkn_bf[:, 0:512], in_=stg[:, t * 512:(t + 1) * 512])
                    for j in range(4):
                        E("pe", "transpose", [QKN, IDB], PBK[TR], out=trb[:, j * 128:(j + 1) * 128], in_=qkn_bf[:, j * 128:(j + 1) * 128], identity=identb[:])
                    E("dve", "tensor_copy", PBK[TR], [KRING[t]], out=kring[:, :, t * 128:(t + 1) * 128], in_=trb[:, 0:512].rearrange("p (a b) -> p a b", a=4))
                DMA("sp", stg.rearrange("p (t n) -> p t n", t=4), cvd[s_].rearrange("(t p) n -> p t n", p=128), [], XG[so], XG[so][0])
                for t in range(4):
                    E("pool", "memset", [], [VRING[t]], vring[:, t, :, 64:65], 1.0)
                    E("pool", "tensor_copy", XG[so], [VRING[t]], out=vring[:, t, :, 0:64], in_=stg[:, t * 512:(t + 1) * 512].rearrange("p (h e) -> p h e", h=8))

        def early_gen(g, under_mixers):
            G = groups[g]
            nkf = 0 if G["sample"] else NT - KEEP
            fresh = G["first"] and not G["sample"]
            getbank = (lambda: SC1) if under_mixers else next_full
            yield from run_group_early(g, G["b"], G["t0"], G["ntl"], G["sample"], G["sidx"], nkf, getbank, fresh)

        def rest_gen(g):
            G = groups[g]
            nkf = 0 if G["sample"] else NT - KEEP
            yield from run_group_rest(g, G["b"], G["t0"], G["ntl"], G["sample"], G["sidx"], nkf, G["last"])

        def step(gen):
            try:
                next(gen)
                return True
            except StopIteration:
                return False

        def exhaust(gen):
            for _ in gen:
                pass

        def can_pipe(g):
            return g + 1 < len(groups) and not groups[g]["sample"] and not groups[g + 1]["sample"]

        flags = {}

        def mixers_gen(g):
            G = groups[g]
            b, t0, ntl, is_sample, sidx = G["b"], G["t0"], G["ntl"], G["sample"], G["sidx"]
            sl, N, bprojs = ginfo[g]
            gp = g % 2
            nkf = 0 if is_sample else NT - KEEP
            slot_of = (lambda t: t) if is_sample else (lambda t: t % RING)
            for ti in range(ntl):
                tseq = t0 + ti
                hook = None
                if ti + 1 < ntl:
                    def nxt(ti=ti, tseq=tseq):
                        gdn_decay(ti + 1, 0, gp=gp)
                        tile_front(b, ti + 1, tseq + 1, bprojs[ti + 1], is_sample, sidx, nkf, banks=[SC0, PV, SC0, PV])
                        gdn_decay(ti + 1, 1, bank=SC0, gp=gp)
                        flags[("hT_free", g)] = True
                    hook = (17, nxt)
                elif ti > 0:
                    hook = (4, lambda ti=ti: tile_back(ti - 1))
                yield from tile_body(b, ti, tseq, 4 if is_sample else tseq, slot_of, is_sample, sidx, nkf, hook, gp)
            flags[("hT_free", g)] = True
            tile_back(ntl - 1)
            if G["last"]:
                dstS = (gdn_s[sidx] if is_sample else gdn_p[b])
                P.stores.append(P.add("sp", lambda e, dstS=dstS: e.dma_start(out=dstS.rearrange("h d e -> d h e"), in_=S32[:].rearrange("p (h e) -> p h e", h=4)), reads=[SB32], dma_buf=SB32))

        def load_group_x(g):
            Gn = groups[g]
            load_x(g, Gn["b"], Gn["t0"], Gn["ntl"], Gn["sample"], Gn["sidx"])

        load_group_x(0)
        state_init(0)
        exhaust(early_gen(0, False))
        exhaust(rest_gen(0))
        tprev = None
        for g, G in enumerate(groups):
            mg = mixers_gen(g)
            pipe = can_pipe(g)
            eg = None
            x_loaded = False
            if pipe and tprev is None:
                load_group_x(g + 1)
                x_loaded = True
            ma = True
            while ma:
                ma = step(mg)
                if tprev is not None and not step(tprev):
                    tprev = None
                    if pipe:
                        load_group_x(g + 1)
                        x_loaded = True
                if pipe and eg is None and x_loaded and flags.get(("hT_free", g)):
                    eg = early_gen(g + 1, True)
                if eg is not None and eg is not False:
                    if not step(eg):
                        eg = False
            if tprev is not None:
                exhaust(tprev)
                tprev = None
            tg = group_tail(G["b"], ginfo[g][0], G["ntl"], G["t0"], G["sample"], G["sidx"])
            if g + 1 < len(groups):
                if pipe:
                    if not x_loaded:
                        load_group_x(g + 1)
                    if eg is None:
                        eg = early_gen(g + 1, False)
                    state_init(g + 1)
                    fa = True
                    if eg is not False:
                        while step(eg):
                            step(tg)
                    rg = rest_gen(g + 1)
                    tmode["fast"] = True
                    while step(rg):
                        step(tg)
                        step(tg)
                    tmode["fast"] = False
                    tprev = tg
                else:
                    exhaust(tg)
                    load_group_x(g + 1)
                    state_init(g + 1)
                    exhaust(early_gen(g + 1, False))
                    exhaust(rest_gen(g + 1))
            else:
                exhaust(tg)

        print('SBUF bytes remaining', nc.sbuf_bytes_remaining() if callable(nc.sbuf_bytes_remaining) else nc.sbuf_bytes_remaining)
        P.cut = False
        P.add("sp", lambda e: e.nop(), extra_deps=[x for x in P.stores if x is not None])
        stats = P.emit()
        build_nc.stats = stats
    return nc


def _consts():
    i = np.arange(128)
    ident = np.eye(128, dtype=np.float32)
    UPi = (i[:, None] <= i[None, :]).astype(np.float32)
    LOWs = (i[:, None] > i[None, :]).astype(np.float32)
    LOWi = (i[:, None] >= i[None, :]).astype(np.float32)
    ones = np.ones((128, 128), np.float32)
    return np.ascontiguousarray(np.stack([ident, UPi, LOWs, LOWi, ones], axis=1))


def _gmask():
    i = np.arange(128)
    ms = [((i[:, None] // 8) == (i[None, :] // 8)) & (i[:, None] > i[None, :])]
    for sz in (8, 16, 32, 64):
        ms.append(((i[:, None] // (2 * sz)) == (i[None, :] // (2 * sz))) & ((i[:, None] // sz) == (i[None, :] // sz) + 1))
    return np.ascontiguousarray(np.stack([m.astype(np.float32) for m in ms], axis=1))


def _mask():
    m = np.zeros((128, 5, 128), np.float32)
    m[0:64, 4, 64:128] = NEG
    m[64:128, 0, 0:64] = NEG
    return m


def _bias_index():
    ki = np.arange(128)[:, None, None]
    dt = np.arange(5)[None, :, None]
    qi = np.arange(128)[None, None, :]
    d = np.clip(dt * 128 + qi - ki, -63, 256) + 63
    return d


_NC_CACHE = {}


def kernel(x_prompt, x_sample, state_conv, state_gdn, cache_k_band, cache_v_band, c_prompt, c_sample,
           w_mod, b_mod, norm1_g, norm2_g, w_in, conv_w, a_log, dt_bias, gdn_norm_g, qn_g, kn_g,
           rel_bias, w_out, w_up, w_down):
    f = lambda a: np.ascontiguousarray(np.asarray(a, dtype=np.float32))
    x_prompt, x_sample = f(x_prompt), f(x_sample)
    B, T, _ = x_prompt.shape
    SB = x_sample.shape[0]
    npq, nsq = B // NCORES, SB // NCORES
    key = (npq, T, nsq)
    if key not in _NC_CACHE:
        _NC_CACHE[key] = build_nc(npq, T, nsq)
    nc = _NC_CACHE[key]
    idx = _bias_index()
    rb = f(rel_bias)[0]
    biasT = np.ascontiguousarray(np.transpose(rb[:, idx], (1, 2, 0, 3)))
    shared = {
        "w_mod": f(w_mod)[0], "bmodT": np.ascontiguousarray(f(b_mod)[0].reshape(48, 128).T),
        "n1gT": np.ascontiguousarray(f(norm1_g)[0].reshape(8, 128).T), "n2gT": np.ascontiguousarray(f(norm2_g)[0].reshape(8, 128).T),
        "w_in": f(w_in)[0], "cwT": np.ascontiguousarray(np.transpose(f(conv_w)[0].reshape(4, 12, 128), (2, 1, 0))),
        "alog": f(a_log).reshape(1, 4), "dtb": f(dt_bias).reshape(1, 4), "gng": f(gdn_norm_g).reshape(1, 128),
        "qkg": np.ascontiguousarray(np.concatenate([f(qn_g)[0], f(kn_g)[0]]).reshape(1, 128)),
        "biasT": biasT, "maskT": _mask(), "w_out": f(w_out)[0], "w_up": f(w_up)[0], "w_down": f(w_down)[0],
        "consts": _consts(), "valid16": (np.arange(128) < 16).astype(np.float32).reshape(128, 1), "gmask": _gmask(),
    }
    sc, sg = f(state_conv)[0], f(state_gdn)[0]
    ck, cv = f(cache_k_band)[0].reshape(SB, 512, 512), f(cache_v_band)[0].reshape(SB, 512, 512)
    cp, cs = f(c_prompt), f(c_sample)
    in_maps = []
    for c in range(NCORES):
        ps, ss = slice(c * npq, (c + 1) * npq), slice(c * nsq, (c + 1) * nsq)
        cvec = np.concatenate([cp[ps], cs[ss]], axis=0)
        cT = np.ascontiguousarray(np.transpose(cvec.reshape(npq + nsq, 8, 128), (2, 1, 0)))
        m = dict(shared)
        m.update({"xp": x_prompt[ps], "xs": x_sample[ss], "sconv": sc[ss], "sgdn": sg[ss], "ck": ck[ss], "cv": cv[ss], "cT": cT})
        in_maps.append(m)
    res = run_bass_kernel_spmd(nc, in_maps, core_ids=list(range(NCORES)))
    R = res.results
    cat = lambda k: np.concatenate([np.asarray(r[k], dtype=np.float32) for r in R], axis=0)
    keep = min(512, T)
    return (cat("y_p"), cat("y_s"), cat("conv_p")[None], cat("gdn_p")[None],
            cat("kb_p").reshape(1, B, keep, 8, 64), cat("vb_p").reshape(1, B, keep, 8, 64),
            cat("conv_s")[None], cat("gdn_s")[None],
            cat("kn_s").reshape(1, SB, 16, 8, 64), cat("vn_s").reshape(1, SB, 16, 8, 64))
```

```python
from contextlib import ExitStack
import numpy as np
import concourse.bass as bass
import concourse.mybir as mybir
from concourse.ap import AP
from concourse.bass_utils import run_bass_kernel_spmd

F32 = mybir.dt.float32
BF16 = mybir.dt.bfloat16
AF = mybir.ActivationFunctionType
ALU = mybir.AluOpType
AX = mybir.AxisListType

D = 1024
NCORES = 8
IN_COLS = 3592
GA_, BA_, AA_, QB_, KB_, VB_ = 1536, 2048, 2052, 2056, 2568, 3080
EPS = 1e-6
RING = 5
NEG = -30000.0


class Buf:
    __slots__ = ("name", "writers", "readers", "slot", "excl")

    def __init__(self, name, excl=False):
        self.name = name
        self.writers = {}
        self.readers = {}
        self.slot = None
        self.excl = excl


class Slot:
    __slots__ = ("sem", "count", "name")

    def __init__(self, sem, name):
        self.sem = sem
        self.count = 0
        self.name = name


class Op:
    __slots__ = ("eng", "fn", "deps", "needs_inc", "slot", "count", "is_dma", "idx")

    def __init__(self, eng, fn, is_dma):
        self.eng = eng
        self.fn = fn
        self.deps = []
        self.needs_inc = False
        self.slot = None
        self.count = None
        self.is_dma = is_dma


ENGS = ("pe", "act", "dve", "pool", "sp")
ENGOBJ = {"pe": "tensor", "act": "scalar", "dve": "vector", "pool": "gpsimd", "sp": "sync"}


class Prog:
    def __init__(self, nc, stack):
        self.nc = nc
        self.stack = stack
        self.ops = {e: [] for e in ENGS}
        self.eng_slot = {e: self.new_slot("eng_" + e) for e in ENGS}
        self.nops = 0
        self.stores = []
        self.cut = False

    def new_slot(self, name):
        sem = self.stack.enter_context(self.nc.semaphore(name))
        return Slot(sem, name)

    def add(self, eng, fn, reads=(), writes=(), dma_buf=None, extra_deps=()):
        if self.cut:
            return None
        is_dma = dma_buf is not None
        op = Op(eng, fn, is_dma)
        op.idx = self.nops
        self.nops += 1
        deps = {}
        for b in reads:
            for d in b.writers.values():
                deps[id(d)] = d
            if b.excl:
                for d in b.readers.values():
                    if d.eng != eng:
                        deps[id(d)] = d
        for b in writes:
            for d in b.writers.values():
                deps[id(d)] = d
            for d in b.readers.values():
                deps[id(d)] = d
        for d in extra_deps:
            deps[id(d)] = d
        for d in deps.values():
            if d.eng == eng and eng == "pe" and not d.is_dma:
                continue
            op.deps.append(d)
            d.needs_inc = True
        key = ("dma", op.idx) if is_dma else eng
        for b in reads:
            b.readers[key] = op
        for b in writes:
            b.writers = {key: op}
            b.readers = {}
        if is_dma:
            if dma_buf.slot is None:
                dma_buf.slot = {}
            if eng not in dma_buf.slot:
                dma_buf.slot[eng] = self.new_slot("d%s_%s" % (eng, dma_buf.name))
            op.slot = dma_buf.slot[eng]
            op.slot.count += 16
            op.count = op.slot.count
            op.needs_inc = True
        else:
            op.slot = self.eng_slot[eng]
        self.ops[eng].append(op)
        return op

    def emit(self):
        nc = self.nc
        for e in ENGS:
            c = 0
            for op in self.ops[e]:
                if op.is_dma:
                    continue
                if op.needs_inc:
                    c += 1
                    op.count = c
        stats = {}
        with nc.Block() as blk:
            for e in ENGS:
                ops = self.ops[e]
                if not ops:
                    continue

                def body(eng, ops=ops, e=e):
                    waited = {}
                    nw = 0
                    for op in ops:
                        need = {}
                        for d in op.deps:
                            k = id(d.slot)
                            if waited.get(k, 0) >= d.count:
                                continue
                            if k not in need or need[k][1] < d.count:
                                need[k] = (d.slot, d.count)
                        for k, (slot, cnt) in need.items():
                            eng.wait_ge(slot.sem, cnt)
                            waited[k] = cnt
                            nw += 1
                        ins = op.fn(eng)
                        if op.needs_inc:
                            ins.then_inc(op.slot.sem, 16 if op.is_dma else 1)
                    stats[e] = (len(ops), nw)

                getattr(blk, ENGOBJ[e])(body)
        return stats


def ap3(t, part, dims, off=0):
    base = t[:]
    pstep = base.ap[0][0]
    return AP(base.tensor, base.offset + off, [[pstep, part]] + [list(d) for d in dims])


def build_nc(n_pseq=4, T=2048, n_sseq=2, dbg=False):
    NT = T // 128
    NB = n_pseq + n_sseq
    KEEP = min(512, T) // 128
    nc = bass.Bass("TRN2", target_bir_lowering=False)

    def din(name, shape, dt=F32):
        return nc.dram_tensor(name, list(shape), dt, kind="ExternalInput").ap()

    def dout(name, shape, dt=F32):
        return nc.dram_tensor(name, list(shape), dt, kind="ExternalOutput").ap()

    xp = din("xp", [n_pseq, T, D])
    xs = din("xs", [max(n_sseq, 1), 16, D])
    sconv = din("sconv", [max(n_sseq, 1), 3, 1536])
    sgdn = din("sgdn", [max(n_sseq, 1), 4, 128, 128])
    ckd = din("ck", [max(n_sseq, 1), 512, 512])
    cvd = din("cv", [max(n_sseq, 1), 512, 512])
    cT = din("cT", [128, 8, NB])
    w_mod = din("w_mod", [D, 6 * D])
    bmodT = din("bmodT", [128, 48])
    n1gT = din("n1gT", [128, 8])
    n2gT = din("n2gT", [128, 8])
    w_in = din("w_in", [D, IN_COLS])
    cwT = din("cwT", [128, 12, 4])
    alog = din("alog", [1, 4])
    dtb_d = din("dtb", [1, 4])
    gng_d = din("gng", [1, 128])
    qkg_d = din("qkg", [1, 128])
    biasd = din("biasT", [128, 5, 8, 128])
    maskd = din("maskT", [128, 5, 128])
    w_out = din("w_out", [D, D])
    w_up = din("w_up", [D, 4 * D])
    w_down = din("w_down", [4 * D, D])
    constd = din("consts", [128, 5, 128])
    validd = din("valid16", [128, 1])
    gmaskd = din("gmask", [128, 5, 128])

    y_p = dout("y_p", [n_pseq, T, D])
    y_s = dout("y_s", [max(n_sseq, 1), 16, D])
    conv_p = dout("conv_p", [n_pseq, 3, 1536])
    gdn_p = dout("gdn_p", [n_pseq, 4, 128, 128])
    kb_p = dout("kb_p", [n_pseq, KEEP * 128, 512])
    vb_p = dout("vb_p", [n_pseq, KEEP * 128, 512])
    conv_s = dout("conv_s", [max(n_sseq, 1), 3, 1536])
    gdn_s = dout("gdn_s", [max(n_sseq, 1), 4, 128, 128])
    kn_s = dout("kn_s", [max(n_sseq, 1), 16, 512])
    vn_s = dout("vn_s", [max(n_sseq, 1), 16, 512])

    wo_s = nc.dram_tensor("wo_s", [4, 128, 2048], BF16, kind="Internal").ap()
    wa_s = nc.dram_tensor("wa_s", [6, 128, 2048], BF16, kind="Internal").ap()
    wu_s = nc.dram_tensor("wu_s", [16, 128, 2048], BF16, kind="Internal").ap()
    wd_s = nc.dram_tensor("wd_s", [16, 128, 2048], BF16, kind="Internal").ap()

    st = ExitStack()
    with st:
        P = Prog(nc, st)

        def sb(name, shape, dt):
            return st.enter_context(nc.sbuf_tensor(name, list(shape), dt))

        def E(eng, meth, reads, writes, *a, **kw):
            return P.add(eng, lambda e: getattr(e, meth)(*a, **kw), reads=reads, writes=writes)

        def DMA(eng, out, in_, reads, writes, buf, **kw):
            return P.add(eng, lambda e: e.dma_start(out=out, in_=in_, **kw), reads=reads, writes=writes, dma_buf=buf)

        WB0 = GA_
        win_sb = sb("win_sb", [128, 8, IN_COLS - WB0], BF16); WIN = [Buf("win%d" % k) for k in range(8)]
        NWR = 3
        wring = [sb("wring%d" % i, [128, 2048], BF16) for i in range(NWR)]
        WRING = [Buf("wring%d" % i) for i in range(NWR)]
        xg = [sb("xg%d" % i, [128, 2, D], F32) for i in range(2)]
        XG = [[Buf("xg%d_%d" % (i, j)) for j in range(2)] for i in range(2)]
        hT = sb("hT", [128, 8, 256], BF16); HT = [Buf("hT%d" % j) for j in range(2)]
        h2T = sb("h2T", [128, 8, 256], BF16); H2T = [Buf("h2T%d" % j) for j in range(2)]
        ffTt = sb("ffTt", [128, 32 * 256], BF16); FFB = Buf("ffT")
        R1 = sb("R1", [128, 4644], F32)
        R1b = R1.bitcast(BF16)
        PCB = [Buf("pc%d" % j) for j in range(12)]
        QKB = [Buf("qk%d" % j) for j in range(12)]
        R1ALL = PCB + QKB

        def pc(j, c0, c1):
            return R1[:, j * 259 + c0: j * 259 + c1]

        def qkvT(j, c0, c1):
            return R1b[:, 6216 + j * 256 + c0: 6216 + j * 256 + c1]

        def ffT(f, c0, c1):
            return ffTt[:, f * 256 + c0: f * 256 + c1]

        cacc = [sb("cacc%d" % i, [128, 256], F32) for i in range(2)]; CACC = [Buf("cacc%d" % i) for i in range(2)]
        ctnh = [sb("ctnh%d" % i, [128, 256], F32) for i in range(2)]; CTNH = [Buf("ctnh%d" % i) for i in range(2)]
        sqT = sb("sqT", [128, 8, 256], BF16); SQT = Buf("sqT")
        bsb = sb("bsb", [128, 1536], F32); BSB = [Buf("bsb_q"), Buf("bsb_k"), Buf("bsb_v")]
        xn_bf = sb("xn_bf", [128, D], BF16); XN = Buf("xn")
        qkn_bf = sb("qkn_bf", [128, 1024], BF16); QKN = Buf("qkn")
        qTz = [sb("qTz%d" % i, [128, 4, 128], BF16) for i in range(2)]; QT = Buf("qT")
        kring = sb("kring", [128, 4, RING * 128], BF16); KRING = [Buf("kr%d" % i) for i in range(RING)]
        vring = sb("vring", [128, RING, 8, 65], BF16); VRING = [Buf("vr%d" % i) for i in range(RING)]
        gsil2 = [sb("gsil%d" % i, [128, 512], F32) for i in range(2)]; GSIL2 = [Buf("gsil%d" % i) for i in range(2)]
        sc2 = sb("sc2", [128, 4, 48], F32); SCA = [Buf("sca_%d" % i) for i in range(4)]; SCN = [Buf("scn_%d" % i) for i in range(4)]
        small = sb("small", [128, 128], F32); SMG = Buf("smallg"); SMA = Buf("smalla")
        smn = sb("smn", [128, 4], F32); SMN = Buf("smn")
        cctx = sb("cctx", [128, 12, 3], F32); CCTX = Buf("cctx")
        kn3 = sb("kn3", [128, 3, 512], BF16); KN3 = Buf("kn3")
        vbt = sb("vbt", [128, 512], BF16); VBT = Buf("vbt")
        knT = sb("knT", [128, 512], BF16); KNT = Buf("knT")
        em = sb("em", [128, 512], F32); ESB = Buf("Esb"); MBB = Buf("MB")
        Esb = em[:, 0:512]
        mbt = sb("mbt", [128, 512], BF16)
        MBt = mbt[:, 0:512]
        t1 = sb("t1", [128, 512], F32); T1 = Buf("t1")
        gU = sb("gU", [128, 512], F32); GU = Buf("gU")
        gbf = [sb("gbf%d" % i, [128, 512], BF16) for i in range(4)]; GBF = [Buf("gbf%d" % i) for i in range(4)]
        gbf.insert(2, gbf[3]); GBF.insert(2, GBF[3])
        gmb = sb("gmb", [128, 5, 128], BF16); GMB = Buf("gmb")
        NTb = [sb("NTb%d" % i, [128, 512], BF16) for i in range(2)]; NTB = [Buf("NTb%d" % i) for i in range(2)]
        NQ = [sb("NQ0", [128, 1024], BF16), sb("NQ1", [128, 512], BF16)]; NQB = [Buf("NQ%d" % i) for i in range(2)]
        qkb = sb("qkb", [128, 512], BF16); QKBF = Buf("qkb")
        Yb = [sb("Yb%d" % i, [128, 512], BF16) for i in range(2)]; YB = [Buf("Yb%d" % i) for i in range(2)]
        nwT = sb("nwT", [128, 512], BF16); NWT = Buf("nwT")
        vnew = sb("vnew", [128, 512], BF16); VNEW = Buf("vnew")
        S32 = sb("S32", [128, 512], F32); SB32 = Buf("S32")
        Sbf = sb("Sbf", [128, 512], BF16); SBF = Buf("Sbf")
        ot = sb("ot", [128, 512], F32); OT = Buf("ot")
        mixb2 = [sb("mix_bf%d" % i, [128, D], BF16) for i in range(2)]; MIXA2 = [Buf("mixA%d" % i) for i in range(2)]
        MIXB2 = [[Buf("mixB%d_%d" % (i, j)) for j in range(2)] for i in range(2)]
        mixT = sb("mixT", [128, 8, 256], BF16); MIXT = [Buf("mixT%d" % j) for j in range(2)]
        scb = [sb("scb%d" % i, [128, 512], F32) for i in range(2)]; SCB = [Buf("scb%d" % i) for i in range(2)]
        Pb0 = sb("Pb0", [128, 5, 512], BF16); Pb = [Pb0, Pb0]; PB0 = [Buf("Pb_%d" % d) for d in range(5)]; PB = [PB0, PB0]
        yT_sb = [sb("yT%d" % i, [128, 256], F32) for i in range(2)]; YT = [Buf("yT%d" % i) for i in range(2)]
        fft = [sb("fft%d" % i, [128, 256], F32) for i in range(2)]; FFT = [Buf("fft%d" % i) for i in range(2)]
        bias_sb = sb("bias_sb", [128, 5, 8, 128], BF16); BIAS = Buf("bias")
        cst = sb("cst", [128, 5, 128], F32); CST = Buf("cst")
        identb = sb("identb", [128, 128], BF16); IDB = Buf("identb")
        gng4 = sb("gng4", [128, 128], F32); GNG = Buf("gng4")
        qkg8 = sb("qkg8", [128, 128], F32); QKG = Buf("qkg8")
        misc = sb("misc", [128, 64], F32); MISC = Buf("misc")
        modT = sb("modT", [128, 48, NB], F32); MODT = Buf("modT")
        hsc = sb("hsc", [128, 2, 8, NB], F32); HSC = Buf("hsc")
        scT = sb("scT", [128, 8, NB], F32); SCT = Buf("scT")
        vecs = sb("vecs", [128, 48 + 16], F32); VECS = Buf("vecs")
        cw = sb("cw", [128, 12, 4], F32); CW = Buf("cw")

        MH = misc[:, 0:1]
        VAL = misc[:, 1:2]
        ident = cst[:, 0, :]
        UPi = cst[:, 1, :]
        LOWs = cst[:, 2, :]
        LOWi = cst[:, 3, :]
        ones = cst[:, 4, :]

        pbk = [st.enter_context(nc.psum_tensor("pb%d" % i, [128, 512], F32)) for i in range(8)]
        PBK = [[Buf("pb%d" % i, excl=True)] for i in range(8)]
        TR, PJ, GA, GB, GC, SC0, SC1, PV = range(8)
        pool_banks = [PJ, GA, GB, GC, SC0, SC1, PV]
        rr = {"full": 0, "half": 0}

        def next_full():
            b = pool_banks[rr["full"] % len(pool_banks)]
            rr["full"] += 1
            return b

        def next_half():
            i = rr["half"] % (2 * len(pool_banks))
            rr["half"] += 1
            return pool_banks[i // 2], i % 2

        sp_q = "sp"
        DMA(sp_q, cst[:], constd, [], [CST], CST)
        E("act", "copy", [CST], [IDB], out=identb[:], in_=ident)
        DMA(sp_q, misc[:, 1:2], validd, [], [MISC], MISC)
        for (c0, src) in ((4, dtb_d), (12, alog)):
            DMA(sp_q, misc[:, c0:c0 + 4], AP(src.tensor, src.offset, [[0, 128], [1, 4]]), [], [MISC], MISC)
        E("pool", "memset", [], [MISC], misc[:, 0:1], -0.5)
        E("act", "activation", [MISC], [MISC], out=misc[:, 8:12], in_=misc[:, 12:16], func=AF.Exp)
        E("dve", "tensor_scalar", [MISC], [MISC], out=misc[:, 8:12], in0=misc[:, 8:12], scalar1=-1.0, scalar2=None, op0=ALU.mult)
        DMA(sp_q, gng4[:, 0:128], AP(gng_d.tensor, gng_d.offset, [[0, 128], [1, 128]]), [], [GNG], GNG)
        E("dve", "tensor_scalar", [GNG], [GNG], out=gng4[:, 0:128], in0=gng4[:, 0:128], scalar1=0.5, scalar2=None, op0=ALU.mult)
        DMA(sp_q, qkg8[:, 0:128], AP(qkg_d.tensor, qkg_d.offset, [[0, 128], [1, 128]]), [], [QKG], QKG)
        DMA(sp_q, cw[:], cwT, [], [CW], CW)
        E("dve", "tensor_scalar", [CW], [CW], out=cw[:], in0=cw[:], scalar1=0.5, scalar2=None, op0=ALU.mult)
        P.add("pool", lambda e: e.dma_start(out=bias_sb[:], in_=biasd), writes=[BIAS], dma_buf=BIAS)
        DMA(sp_q, bsb[:, 0:640], maskd.rearrange("p a b -> p (a b)"), [], BSB, BSB[0])
        E("dve", "tensor_tensor", [BIAS] + BSB, [BIAS], out=bias_sb[:],
          in0=bias_sb[:], in1=ap3(bsb, 128, [[128, 5], [0, 8], [1, 128]]), op=ALU.add)
        P.add("pool", lambda e: e.dma_start(out=gmb[:], in_=gmaskd), writes=[GMB], dma_buf=GMB)
        for i in range(2):
            E("pool", "memset", [], [QT], qTz[i][:], 0.0)
        E("pool", "memset", [], VRING, vring[:], 1.0)
        DMA(sp_q, vecs[:, 0:48], bmodT, [], [VECS], VECS)
        DMA(sp_q, vecs[:, 48:56], n1gT, [], [VECS], VECS)
        DMA(sp_q, vecs[:, 56:64], n2gT, [], [VECS], VECS)
        DMA(sp_q, scT[:], cT, [], [SCT], SCT)
        scf = scT[:].rearrange("p a b -> p (a b)")
        E("act", "activation", [SCT], [SMG], out=small[:, 0:8 * NB], in_=scf, func=AF.Tanh, scale=0.5)
        E("dve", "scalar_tensor_tensor", [SMG, SCT], [SCT], out=scf, in0=small[:, 0:8 * NB], scalar=1.0, in1=scf, op0=ALU.add, op1=ALU.mult)
        E("dve", "tensor_scalar", [SCT], [SCT], out=scf, in0=scf, scalar1=0.5, scalar2=None, op0=ALU.mult)
        mod_ps = pbk[PJ]
        for pi in range(24):
            sl = pi % 2
            wm = xg[sl][:].rearrange("p a b -> p (a b)")
            DMA(sp_q, wm.rearrange("p (k n) -> p k n", k=8), w_mod[:, pi * 256:(pi + 1) * 256].rearrange("(k p) n -> p k n", p=128),
                [], XG[sl], XG[sl][0])
            for c2 in range(2):
                cc = pi * 2 + c2
                for k in range(8):
                    E("pe", "matmul", XG[sl] + [SCT], PBK[PJ], mod_ps[:, cc * NB:(cc + 1) * NB],
                      lhsT=wm[:, k * 256 + c2 * 128: k * 256 + (c2 + 1) * 128], rhs=scT[:, k, :], start=(k == 0), stop=(k == 7))
        E("dve", "tensor_tensor", PBK[PJ] + [VECS], [MODT], out=modT[:], in0=mod_ps[:, 0:48 * NB].rearrange("p (a b) -> p a b", b=NB),
          in1=ap3(vecs, 128, [[1, 48], [0, NB]]), op=ALU.add)
        for n, (c0, g0) in enumerate(((8, 48), (32, 56))):
            E("dve", "scalar_tensor_tensor", [MODT, VECS], [HSC], out=hsc[:, n], in0=modT[:, c0:c0 + 8, :], scalar=1.0,
              in1=ap3(vecs, 128, [[1, 8], [0, NB]], off=g0), op0=ALU.add, op1=ALU.mult)

        def hscale(n, j, b):
            return hsc[:, n, j, b:b + 1]

        def mvec(idx, j, b):
            return modT[:, idx * 8 + j, b:b + 1]

        for k in range(8):
            P.add("pool", lambda e, k=k: e.dma_start(out=win_sb[:, k, :], in_=w_in[k * 128:(k + 1) * 128, WB0:IN_COLS]), writes=[WIN[k]], dma_buf=WIN[k])
        WSCR = {}
        blocks = []
        for i in range(6):
            blocks.append((("a", i), w_in[:, i * 256:(i + 1) * 256].rearrange("(k p) n -> p k n", p=128), "p (k n) -> p k n", 8, wa_s[i]))
        for i in range(4):
            blocks.append((("o", i), w_out[:, i * 256:(i + 1) * 256].rearrange("(k p) n -> p k n", p=128), "p (k n) -> p k n", 8, wo_s[i]))
        for i in range(16):
            blocks.append((("u", i), w_up[:, i * 256:(i + 1) * 256].rearrange("(k p) n -> p k n", p=128), "p (k n) -> p k n", 8, wu_s[i]))
        for dc in range(8):
            for fh in range(2):
                blocks.append((("d", dc * 2 + fh), w_down[fh * 2048:(fh + 1) * 2048, dc * 128:(dc + 1) * 128].rearrange("(f p) n -> p f n", p=128),
                               "p (f n) -> p f n", 16, wd_s[dc * 2 + fh]))
        for bi, (key, src, pat, a0, scr) in enumerate(blocks):
            sl = bi % NWR
            WSCR[key] = Buf("wscr_%s%d" % key)
            dstv = wring[sl][:].rearrange(pat, **({"k": a0} if "k" in pat else {"f": a0}))
            P.add("pool", lambda e, dstv=dstv, src=src: e.dma_start(out=dstv, in_=src), writes=[WRING[sl]], dma_buf=WRING[sl])
            P.add("sp", lambda e, scr=scr, sl=sl: e.dma_start(out=scr, in_=wring[sl][:]), reads=[WRING[sl]], writes=[WSCR[key]], dma_buf=WRING[sl])

        def wload(key, scr):
            ws = gstate.setdefault("wcnt", 0)
            gstate["wcnt"] = ws + 1
            rs = ws % NWR
            DMA("sp", wring[rs][:], scr, [WSCR[key]], [WRING[rs]], WRING[rs])
            return rs

        def norm_to_hT_gen(sl, ti, n, b):
            xt = xg[sl][:, ti, :]
            dT, DTB = (hT, HT) if n == 0 else (h2T, H2T)
            E("act", "activation", [XG[sl][ti]], [XN, SMN], out=xn_bf[:], in_=xt, func=AF.Square, accum_out=smn[:, 0:1])
            yield
            E("pool", "tensor_scalar", [SMN], [SMN], out=smn[:, 1:2], in0=smn[:, 0:1], scalar1=1.0 / D, scalar2=EPS, op0=ALU.mult, op1=ALU.add)
            E("pool", "tensor_tensor", [SMN, MISC], [SMN], out=smn[:, 1:2], in0=smn[:, 1:2], in1=MH, op=ALU.pow)
            yield
            E("act", "activation", [XG[sl][ti], SMN], [XN], out=xn_bf[:], in_=xt, func=AF.Copy, scale=smn[:, 1:2])
            yield
            yield
            trb = pbk[TR].bitcast(BF16)
            for j in range(8):
                E("pe", "transpose", [XN, IDB], PBK[TR], out=trb[:, j * 128:(j + 1) * 128], in_=xn_bf[:, j * 128:(j + 1) * 128], identity=identb[:])
            for j in range(8):
                if (ti + n) % 2 == 0:
                    E("act", "activation", PBK[TR] + [HSC, MODT], [DTB[ti]], out=dT[:, j, ti * 128:(ti + 1) * 128], in_=trb[:, j * 128:(j + 1) * 128],
                      func=AF.Identity, scale=hscale(n, j, b), bias=mvec(3 * n, j, b))
                else:
                    E("dve", "tensor_scalar", PBK[TR] + [HSC, MODT], [DTB[ti]], out=dT[:, j, ti * 128:(ti + 1) * 128], in0=trb[:, j * 128:(j + 1) * 128],
                      scalar1=hscale(n, j, b), scalar2=mvec(3 * n, j, b), op0=ALU.mult, op1=ALU.add)
            yield

        def norm_to_hT(sl, ti, n, b):
            xt = xg[sl][:, ti, :]
            dT, DTB = (hT, HT) if n == 0 else (h2T, H2T)
            E("act", "activation", [XG[sl][ti]], [XN, SMN], out=xn_bf[:], in_=xt, func=AF.Square, accum_out=smn[:, 0:1])
            E("pool", "tensor_scalar", [SMN], [SMN], out=smn[:, 1:2], in0=smn[:, 0:1], scalar1=1.0 / D, scalar2=EPS, op0=ALU.mult, op1=ALU.add)
            E("pool", "tensor_tensor", [SMN, MISC], [SMN], out=smn[:, 1:2], in0=smn[:, 1:2], in1=MH, op=ALU.pow)
            E("act", "activation", [XG[sl][ti], SMN], [XN], out=xn_bf[:], in_=xt, func=AF.Copy, scale=smn[:, 1:2])
            trb = pbk[TR].bitcast(BF16)
            for j in range(8):
                E("pe", "transpose", [XN, IDB], PBK[TR], out=trb[:, j * 128:(j + 1) * 128], in_=xn_bf[:, j * 128:(j + 1) * 128], identity=identb[:])
            for j in range(8):
                eng = "act" if (ti + n) % 2 == 0 else "dve"
                if eng == "act":
                    E("act", "activation", PBK[TR] + [HSC, MODT], [DTB[ti]], out=dT[:, j, ti * 128:(ti + 1) * 128], in_=trb[:, j * 128:(j + 1) * 128],
                      func=AF.Identity, scale=hscale(n, j, b), bias=mvec(3 * n, j, b))
                else:
                    E("dve", "tensor_scalar", PBK[TR] + [HSC, MODT], [DTB[ti]], out=dT[:, j, ti * 128:(ti + 1) * 128], in0=trb[:, j * 128:(j + 1) * 128],
                      scalar1=hscale(n, j, b), scalar2=mvec(3 * n, j, b), op0=ALU.mult, op1=ALU.add)


        tmode = {"fast": False, "n": 0}

        def tbank():
            if tmode["fast"]:
                tmode["n"] += 1
                return PJ if tmode["n"] % 2 == 0 else SC1
            return PJ

        def dense_back(sl, ntl, b, gate_idx, chunk_mm):
            N = 128 * ntl
            for dp in range(4):
                TB = tbank()
                for half in range(2):
                    dc = dp * 2 + half
                    pst = pbk[TB][:, half * 256: half * 256 + N]
                    yield from chunk_mm(dc, pst, PBK[TB])
                for half in range(2):
                    dc = dp * 2 + half
                    pst = pbk[TB][:, half * 256: half * 256 + N]
                    E("act", "activation", PBK[TB] + [MODT], [YT[half]], out=yT_sb[half][:, 0:N], in_=pst, func=AF.Copy, scale=mvec(gate_idx, dc, b))
                for half in range(2):
                    for ti in range(ntl):
                        E("pe", "transpose", [YT[half], CST], PBK[TB], out=pbk[TB][:, ti * 256 + half * 128: ti * 256 + (half + 1) * 128],
                          in_=yT_sb[half][:, ti * 128:(ti + 1) * 128], identity=ident)
                for ti in range(ntl):
                    E("dve", "tensor_tensor", PBK[TB] + [XG[sl][ti]], [XG[sl][ti]], out=xg[sl][:, ti, dp * 256:(dp + 1) * 256],
                      in0=pbk[TB][:, ti * 256:(ti + 1) * 256], in1=xg[sl][:, ti, dp * 256:(dp + 1) * 256], op=ALU.add)
                yield

        pool_banks.remove(PJ)
        pool_banks.remove(SC1)

        def gdn_decay(ti, part, bank=None, gp=0):
            SC = SCA[gp * 2 + ti]

            def bc4(c):
                return ap3(sc2, 128, [[1, 4], [0, 128]], off=(gp * 2 + ti) * 48 + c)
            if part == 0:
                for h in range(4):
                    E("pool", "tensor_scalar", [SC, CST], [GU], out=gU[:, h * 128:(h + 1) * 128], in0=UPi, scalar1=sc2[:, gp * 2 + ti, 12 + h:13 + h], scalar2=None, op0=ALU.mult)
                E("pool", "tensor_tensor", [SC, CST], [MBB], out=MBt.rearrange("p (h e) -> p h e", h=4), in0=ap3(cst, 128, [[0, 4], [1, 128]], off=2 * 128),
                  in1=bc4(36), op=ALU.mult)
                return
            bk = GC if bank is None else bank
            for h in range(4):
                E("pe", "matmul", [GU, CST], PBK[bk], pbk[bk][:, h * 128:(h + 1) * 128], lhsT=gU[:, h * 128:(h + 1) * 128], rhs=LOWs, start=True, stop=True)
            E("act", "activation", PBK[bk], [ESB], out=Esb, in_=pbk[bk][:], func=AF.Exp)
            E("dve", "tensor_tensor", [ESB, CST], [ESB], out=Esb.rearrange("p (h e) -> p h e", h=4), in0=Esb.rearrange("p (h e) -> p h e", h=4),
              in1=ap3(cst, 128, [[0, 4], [1, 128]], off=3 * 128), op=ALU.mult)

        def gdn_tile(b, ti, is_sample, gp=0):
            c0 = ti * 128

            def qh(j, h):
                return qkvT(j * 4 + h, c0, c0 + 128)
            trb = pbk[TR].bitcast(BF16)
            gsil = gsil2[ti]
            GSIL = GSIL2[ti]
            mix_bf = mixb2[ti]
            MIXA = MIXA2[ti]
            SC = SCA[gp * 2 + ti]
            SCK = SCN[gp * 2 + ti]

            def bc4(c):
                return ap3(sc2, 128, [[1, 4], [0, 128]], off=(gp * 2 + ti) * 48 + c)

            def bcs(c):
                return ap3(small, 128, [[1, 4], [0, 128]], off=c)
            for h in range(4):
                E("pe", "transpose", [QKB[4 + h], IDB], PBK[TR], out=trb[:, h * 128:(h + 1) * 128], in_=qh(1, h), identity=identb[:])
            for h in range(4):
                E("pe", "transpose", [QKB[8 + h], IDB], PBK[TR], out=trb[:, 512 + h * 128: 512 + (h + 1) * 128], in_=qh(2, h), identity=identb[:])
            kt3 = trb[:, 0:512].rearrange("p (h e) -> p h e", h=4)
            vt3 = trb[:, 512:1024].rearrange("p (h e) -> p h e", h=4)
            for i, c in enumerate((4, 28, 32)):
                E("dve", "tensor_tensor", PBK[TR] + [SCK], [KN3], out=kn3[:, i, :].rearrange("p (h e) -> p h e", h=4), in0=kt3, in1=bc4(c), op=ALU.mult)
            E("dve", "tensor_tensor", PBK[TR] + [SC], [VBT], out=vbt[:].rearrange("p (h e) -> p h e", h=4), in0=vt3, in1=bc4(8), op=ALU.mult)
            yield
            for h in range(4):
                E("pe", "transpose", [KN3, IDB], PBK[TR], out=trb[:, h * 128:(h + 1) * 128], in_=kn3[:, 0, h * 128:(h + 1) * 128], identity=identb[:])
            E("act", "copy", PBK[TR], [KNT], out=knT[:], in_=trb[:, 0:512])
            yield
            for h in range(4):
                E("pe", "matmul", [KNT], PBK[GA], pbk[GA][:, h * 128:(h + 1) * 128], lhsT=knT[:, h * 128:(h + 1) * 128], rhs=knT[:, h * 128:(h + 1) * 128], start=True, stop=True)
            for h in range(4):
                E("pe", "matmul", [KNT, QKB[h]], PBK[GB], pbk[GB][:, h * 128:(h + 1) * 128], lhsT=qh(0, h), rhs=knT[:, h * 128:(h + 1) * 128], start=True, stop=True)
            E("dve", "tensor_tensor", PBK[GA] + [ESB], [T1], out=t1[:], in0=pbk[GA][:], in1=Esb, op=ALU.mult)
            E("dve", "tensor_tensor", [T1, MBB], [NTB[0]], out=NTb[0][:], in0=t1[:], in1=MBt, op=ALU.mult)
            E("dve", "tensor_tensor", PBK[GB] + [ESB], [QKBF], out=qkb[:], in0=pbk[GB][:], in1=Esb, op=ALU.mult)
            yield
            def gm(i):
                return ap3(gmb, 128, [[0, 4], [1, 128]], off=i * 128)

            def v4(t):
                return t[:].rearrange("p (h e) -> p h e", h=4)
            nT0, nbT, Xs = NTb[0], NTb[1], NQ[1]
            NT0B, NBTB, XSB = NTB[0], NTB[1], NQB[1]
            E("pool", "tensor_tensor", [NT0B, GMB], [NBTB], out=v4(nbT), in0=v4(nT0), in1=gm(0), op=ALU.mult)
            for h in range(4):
                E("pe", "transpose", [NBTB, IDB], PBK[TR], out=trb[:, h * 128:(h + 1) * 128], in_=nbT[:, h * 128:(h + 1) * 128], identity=identb[:])
            for h in range(4):
                E("pe", "transpose", [QKBF, IDB], PBK[TR], out=trb[:, 512 + h * 128:512 + (h + 1) * 128], in_=qkb[:, h * 128:(h + 1) * 128], identity=identb[:])
            E("act", "copy", PBK[TR], [NQB[0]], out=NQ[0][:], in_=trb[:, 0:1024])
            E("dve", "tensor_tensor", PBK[TR] + [IDB], [YB[0]], out=Yb[0][:].rearrange("p (h e) -> p h e", h=4), in0=trb[:, 0:512].rearrange("p (h e) -> p h e", h=4),
              in1=ap3(identb, 128, [[0, 4], [1, 128]]), op=ALU.add)
            nb = NQ[0][:, 0:512]
            yield

            def mm4(bank, lhs, rhs, rl, rr):
                for h in range(4):
                    E("pe", "matmul", rl + rr, PBK[bank], pbk[bank][:, h * 128:(h + 1) * 128], lhsT=lhs[:, h * 128:(h + 1) * 128], rhs=rhs[:, h * 128:(h + 1) * 128], start=True, stop=True)
            mm4(GA, nbT, nb, [NBTB], [NQB[0]])
            mm4(GB, nb, nbT, [NQB[0]], [NBTB])
            E("act", "copy", PBK[GA], [GBF[0]], out=gbf[0][:], in_=pbk[GA][:])
            E("dve", "tensor_copy", PBK[GB], [GBF[1]], out=gbf[1][:], in_=pbk[GB][:])
            yield
            mm4(GC, gbf[1], Yb[0], [GBF[1]], [YB[0]])
            mm4(GA, gbf[0], gbf[1], [GBF[0]], [GBF[1]])
            E("dve", "tensor_tensor", PBK[GC] + [YB[0]], [YB[1]], out=Yb[1][:], in0=pbk[GC][:], in1=Yb[0][:], op=ALU.add)
            E("act", "copy", PBK[GA], [GBF[2]], out=gbf[2][:], in_=pbk[GA][:])
            yield
            mm4(GC, gbf[2], Yb[1], [GBF[2]], [YB[1]])
            E("dve", "tensor_tensor", PBK[GC] + [YB[1]], [YB[0]], out=Yb[0][:], in0=pbk[GC][:], in1=Yb[1][:], op=ALU.add)
            yield
            cur = 0
            E("pool", "tensor_tensor", [NT0B, GMB], [GBF[3]], out=v4(gbf[3]), in0=v4(nT0), in1=gm(1), op=ALU.mult)
            for li in range(4):
                U, UB = Yb[cur], YB[cur]
                LM, LMB = gbf[3 + li % 2], GBF[3 + li % 2]
                mm4(GA, LM, U, [LMB], [UB])
                if li < 3:
                    E("pool", "tensor_tensor", [NT0B, GMB], [GBF[3 + (li + 1) % 2]], out=v4(gbf[3 + (li + 1) % 2]), in0=v4(nT0), in1=gm(2 + li), op=ALU.mult)
                for h in range(4):
                    E("pe", "transpose", [UB, IDB], PBK[TR], out=trb[:, h * 128:(h + 1) * 128], in_=U[:, h * 128:(h + 1) * 128], identity=identb[:])
                E("act", "copy", PBK[GA], [GBF[0]], out=gbf[0][:], in_=pbk[GA][:])
                E("dve", "tensor_copy", PBK[TR], [XSB], out=Xs[:], in_=trb[:, 0:512])
                yield
                mm4(GC, Xs, gbf[0], [XSB], [GBF[0]])
                E("dve", "tensor_tensor", PBK[GC] + [UB], [YB[1 - cur]], out=Yb[1 - cur][:], in0=pbk[GC][:], in1=U[:], op=ALU.add)
                cur = 1 - cur
                yield
            XT = Yb[cur]
            XTB = YB[cur]
            qkT = NQ[0][:, 512:1024]
            yield
            for h in range(4):
                E("pe", "matmul", [KN3, XTB], PBK[GA], pbk[GA][:, h * 128:(h + 1) * 128], lhsT=kn3[:, 1, h * 128:(h + 1) * 128], rhs=XT[:, h * 128:(h + 1) * 128], start=True, stop=True)
            E("act", "activation", PBK[GA], [NWT], out=nwT[:], in_=pbk[GA][:], func=AF.Copy, scale=-1.0)
            for h in range(4):
                E("pe", "matmul", [QKB[h], SBF], PBK[GC], pbk[GC][:, h * 128:(h + 1) * 128], lhsT=qh(0, h), rhs=Sbf[:, h * 128:(h + 1) * 128], start=True, stop=True)
            yield
            for h in range(4):
                E("pe", "matmul", [XTB, VBT], PBK[GB], pbk[GB][:, h * 128:(h + 1) * 128], lhsT=XT[:, h * 128:(h + 1) * 128], rhs=vbt[:, h * 128:(h + 1) * 128], start=True, stop=False)
                E("pe", "matmul", [NWT, SBF], PBK[GB], pbk[GB][:, h * 128:(h + 1) * 128], lhsT=nwT[:, h * 128:(h + 1) * 128], rhs=Sbf[:, h * 128:(h + 1) * 128], start=False, stop=True)
            E("act", "copy", PBK[GB], [VNEW], out=vnew[:], in_=pbk[GB][:])
            E("dve", "tensor_tensor", PBK[GC] + [SC], [OT], out=ot[:].rearrange("p (h e) -> p h e", h=4), in0=pbk[GC][:].rearrange("p (h e) -> p h e", h=4), in1=bc4(16), op=ALU.mult)
            yield
            for h in range(4):
                E("pe", "matmul", [NQB[0], VNEW], PBK[GA], pbk[GA][:, h * 128:(h + 1) * 128], lhsT=qkT[:, h * 128:(h + 1) * 128], rhs=vnew[:, h * 128:(h + 1) * 128], start=True, stop=True)
            for h in range(4):
                E("pe", "matmul", [KN3, VNEW], PBK[GB], pbk[GB][:, h * 128:(h + 1) * 128], lhsT=kn3[:, 2, h * 128:(h + 1) * 128], rhs=vnew[:, h * 128:(h + 1) * 128], start=True, stop=True)
            E("dve", "tensor_tensor", PBK[GA] + [OT], [OT], out=ot[:], in0=pbk[GA][:], in1=ot[:], op=ALU.add)
            E("dve", "tensor_tensor", [SB32, SC], [SB32], out=S32[:].rearrange("p (h e) -> p h e", h=4), in0=S32[:].rearrange("p (h e) -> p h e", h=4), in1=bc4(24), op=ALU.mult)
            E("dve", "tensor_tensor", PBK[GB] + [SB32], [SB32], out=S32[:], in0=pbk[GB][:], in1=S32[:], op=ALU.add)
            E("act", "copy", [SB32], [SBF], out=Sbf[:], in_=S32[:])
            rq = sc2[:, gp * 2 + ti, 0:4]
            E("act", "activation", [OT], [T1], out=t1[:], in_=ot[:], func=AF.Square)
            E("dve", "tensor_reduce", [T1], [SMG], out=small[:, 44:48], in_=t1[:].rearrange("p (h e) -> p h e", h=4), axis=AX.X, op=ALU.add)
            E("dve", "tensor_tensor", [SCK], [SMG], out=small[:, 60:64], in0=rq, in1=rq, op=ALU.mult)
            E("dve", "tensor_tensor", [SMG], [SMG], out=small[:, 60:64], in0=small[:, 60:64], in1=small[:, 44:48], op=ALU.mult)
            E("dve", "tensor_scalar", [SMG], [SMG], out=small[:, 60:64], in0=small[:, 60:64], scalar1=1.0 / 128, scalar2=EPS, op0=ALU.mult, op1=ALU.add)
            E("pool", "tensor_tensor", [SMG, MISC], [SMG], out=small[:, 60:64], in0=small[:, 60:64], in1=ap3(misc, 128, [[0, 4]]), op=ALU.pow)
            E("dve", "tensor_tensor", [SMG, SCK], [SMG], out=small[:, 48:52], in0=small[:, 60:64], in1=rq, op=ALU.mult)
            E("dve", "tensor_tensor", [OT, SMG], [OT], out=ot[:].rearrange("p (h e) -> p h e", h=4), in0=ot[:].rearrange("p (h e) -> p h e", h=4), in1=bcs(48), op=ALU.mult)
            E("dve", "tensor_tensor", [OT, GNG], [OT], out=ot[:].rearrange("p (h e) -> p h e", h=4), in0=ot[:].rearrange("p (h e) -> p h e", h=4), in1=ap3(gng4, 128, [[0, 4], [1, 128]]), op=ALU.mult)
            E("dve", "tensor_tensor", [OT, GSIL], [MIXA], out=mix_bf[:, 0:512], in0=ot[:], in1=gsil[:], op=ALU.mult)
            yield

        def attn_tile(b, ti, tglob, slot_of):
            trb = pbk[TR].bitcast(BF16)
            mix_bf = mixb2[ti]
            MIXB = MIXB2[ti]
            for i in range(2):
                E("act", "activation", [BSB[i]], [SCB[i]], out=scb[i][:], in_=bsb[:, i * 512:(i + 1) * 512], func=AF.Square)
                E("dve", "tensor_reduce", [SCB[i]], [SMA], out=small[:, 64 + i * 8:72 + i * 8], in_=scb[i][:].rearrange("p (h e) -> p h e", h=8), axis=AX.X, op=ALU.add)
            E("dve", "tensor_scalar", [SMA], [SMA], out=small[:, 64:80], in0=small[:, 64:80], scalar1=1.0 / 64, scalar2=EPS, op0=ALU.mult, op1=ALU.add)
            E("pool", "tensor_tensor", [SMA, MISC], [SMA], out=small[:, 64:80], in0=small[:, 64:80], in1=ap3(misc, 128, [[0, 16]]), op=ALU.pow)
            for i in range(2):
                v8 = bsb[:, i * 512:(i + 1) * 512].rearrange("p (h e) -> p h e", h=8)
                E("dve", "tensor_tensor", [BSB[i], SMA], [BSB[i]], out=v8, in0=v8, in1=ap3(small, 128, [[1, 8], [0, 64]], off=64 + i * 8), op=ALU.mult)
                E("dve", "tensor_tensor", [BSB[i], QKG], [BSB[i]], out=v8, in0=v8, in1=ap3(qkg8, 128, [[0, 8], [1, 64]], off=64 * i), op=ALU.mult)
            yield
            E("act", "copy", [BSB[0], BSB[1]], [QKN], out=qkn_bf[:], in_=bsb[:, 0:1024])
            sl = slot_of(tglob)
            if b >= n_pseq:
                E("pool", "memset", [], [VRING[sl]], vring[:, sl], 0.0)
                E("pool", "tensor_copy", [BSB[2]], [VRING[sl]], out=vring[0:16, sl, :, 0:64], in_=bsb[0:16, 1024:1536].rearrange("p (h e) -> p h e", h=8))
                E("pool", "memset", [], [VRING[sl]], vring[0:16, sl, :, 64:65], 1.0)
            else:
                E("pool", "tensor_copy", [BSB[2]], [VRING[sl]], out=vring[:, sl, :, 0:64], in_=bsb[:, 1024:1536].rearrange("p (h e) -> p h e", h=8))
            for j in range(8):
                E("pe", "transpose", [QKN, IDB], PBK[TR], out=trb[:, j * 128:(j + 1) * 128], in_=qkn_bf[:, j * 128:(j + 1) * 128], identity=identb[:])
            E("act", "activation", PBK[TR], [QT], out=qTz[0][0:64].rearrange("p a b -> p (a b)"), in_=trb[0:64, 0:512], func=AF.Copy, scale=0.125)
            E("act", "activation", PBK[TR], [QT], out=qTz[1][64:128].rearrange("p a b -> p (a b)"), in_=trb[64:128, 0:512], func=AF.Copy, scale=0.125)
            E("dve", "tensor_copy", PBK[TR], [KRING[sl]], out=kring[:, :, sl * 128:(sl + 1) * 128], in_=trb[:, 512:1024].rearrange("p (a b) -> p a b", a=4))
            yield
            dts = [dt for dt in range(5) if tglob - dt >= 0]
            u = 0
            for hh in range(2):
                base = 64 * hh
                for dt in dts:
                    ks = slot_of(tglob - dt)
                    bank = SC0
                    for i in range(4):
                        E("pe", "matmul", [KRING[ks], QT], PBK[bank], pbk[bank][:, i * 128:(i + 1) * 128],
                          lhsT=kring[:, i, ks * 128:(ks + 1) * 128], rhs=qTz[hh][:, i, :], start=(i == 0), stop=False)
                    E("pe", "matmul", [BIAS, IDB], PBK[bank], pbk[bank][:].rearrange("p (h e) -> p h e", h=4), lhsT=identb[:],
                      rhs=ap3(bias_sb, 128, [[256, 4], [1, 128]], off=dt * 1024 + hh * 128), start=False, stop=True)
                    E("act", "activation", PBK[bank], [PB[hh][dt]], out=Pb[hh][:, dt, :], in_=pbk[bank][:], func=AF.Exp)
                    u += 1
                    if u % 2 == 0:
                        yield
                pvv = pbk[PV][:].rearrange("p (h e) -> p h e", h=4)
                for i in range(4):
                    h = 2 * i + hh
                    for n_, dt in enumerate(dts):
                        ks = slot_of(tglob - dt)
                        E("pe", "matmul", [PB[hh][dt], VRING[ks]], PBK[PV], pvv[:, i, 0:65], lhsT=Pb[hh][:, dt, i * 128:(i + 1) * 128],
                          rhs=vring[:, ks, h, :], start=(n_ == 0), stop=(n_ == len(dts) - 1))
                E("dve", "reciprocal", PBK[PV], [SMA], out=small[:, 80 + hh * 4: 84 + hh * 4].rearrange("p (h e) -> p h e", e=1), in_=pvv[:, :, 64:65])
                E("dve", "tensor_tensor", PBK[PV] + [SMA], [MIXB[hh]], out=ap3(mix_bf, 128, [[128, 4], [1, 64]], off=512 + hh * 64),
                  in0=pvv[:, :, 0:64], in1=ap3(small, 128, [[1, 4], [0, 64]], off=80 + hh * 4), op=ALU.mult)
                yield

        onesb = sb("onesb", [128, 2], BF16); ONB = Buf("onesb")
        E("pool", "memset", [], [ONB], onesb[:], 1.0)
        CST_ONB = ONB

        import os
        STOPF = float(os.environ.get("DBG_STOP", "99"))
        gstate = {"g": 0}
        pcv_all = R1[:, 0:12 * 259].rearrange("p (j n) -> p j n", j=12)

        def load_x(g, b, t0, ntl, is_sample, sidx):
            sl = g % 2
            for ti in range(ntl):
                if is_sample:
                    E("pool", "memset", [], [XG[sl][ti]], xg[sl][:, ti, :], 0.0)
                    DMA("sp", xg[sl][0:16, ti, :], xs[sidx], [], [XG[sl][ti]], XG[sl][ti])
                else:
                    DMA("sp", xg[sl][:, ti, :], xp[b, (t0 + ti) * 128:(t0 + ti + 1) * 128, :], [], [XG[sl][ti]], XG[sl][ti])

        ginfo = {}
        early_norm = {}

        def run_group_early(g, b, t0, ntl, is_sample, sidx, nkf, getbank, fresh_ctx):
            sl = g % 2
            gp = g % 2
            N = 128 * ntl
            for ti in range(ntl):
                yield from norm_to_hT_gen(sl, ti, 0, b)
            HTg = HT[0:ntl]
            SCg = SCA[gp * 2: gp * 2 + ntl]
            SCBg = SCN[gp * 2: gp * 2 + ntl]
            bank = getbank()
            for ti in range(ntl):
                for k in range(8):
                    E("pe", "matmul", [HT[ti], WIN[k]], PBK[bank], pbk[bank][:, ti * 8:(ti + 1) * 8], lhsT=hT[:, k, ti * 128:(ti + 1) * 128], rhs=win_sb[:, k, BA_ - WB0:QB_ - WB0],
                      start=(k == 0), stop=(k == 7))
            yield
            yield
            E("dve", "tensor_copy", PBK[bank], SCg, out=sc2[:, gp * 2: gp * 2 + ntl, 40:48], in_=pbk[bank][:, 0:ntl * 8].rearrange("p (t c) -> p t c", t=ntl))
            E("act", "activation", SCg, SCg, out=sc2[:, gp * 2: gp * 2 + ntl, 8:12], in_=sc2[:, gp * 2: gp * 2 + ntl, 40:44], func=AF.Tanh, scale=0.5)
            E("dve", "tensor_scalar", SCg, SCg, out=sc2[:, gp * 2: gp * 2 + ntl, 8:12], in0=sc2[:, gp * 2: gp * 2 + ntl, 8:12], scalar1=0.5, scalar2=0.5, op0=ALU.mult, op1=ALU.add)
            E("dve", "tensor_tensor", SCg + [MISC], SCg, out=sc2[:, gp * 2: gp * 2 + ntl, 12:16], in0=sc2[:, gp * 2: gp * 2 + ntl, 44:48], in1=ap3(misc, 128, [[0, ntl], [1, 4]], off=4), op=ALU.add)
            E("act", "activation", SCg, SCg, out=sc2[:, gp * 2: gp * 2 + ntl, 12:16], in_=sc2[:, gp * 2: gp * 2 + ntl, 12:16], func=AF.Exp)
            E("act", "activation", SCg, SCg, out=sc2[:, gp * 2: gp * 2 + ntl, 12:16], in_=sc2[:, gp * 2: gp * 2 + ntl, 12:16], func=AF.Ln, bias=1.0)
            E("dve", "tensor_tensor", SCg + [MISC], SCg, out=sc2[:, gp * 2: gp * 2 + ntl, 12:16], in0=sc2[:, gp * 2: gp * 2 + ntl, 12:16], in1=ap3(misc, 128, [[0, ntl], [1, 4]], off=8), op=ALU.mult)
            if is_sample:
                E("dve", "tensor_scalar", SCg + [MISC], SCg, out=sc2[:, gp * 2: gp * 2 + ntl, 8:16], in0=sc2[:, gp * 2: gp * 2 + ntl, 8:16], scalar1=VAL, scalar2=None, op0=ALU.mult)
            E("dve", "tensor_scalar", SCg, SCg, out=sc2[:, gp * 2: gp * 2 + ntl, 36:40], in0=sc2[:, gp * 2: gp * 2 + ntl, 8:12], scalar1=-1.0, scalar2=None, op0=ALU.mult)
            yield
            yield
            bank = getbank()
            for ti in range(ntl):
                for i, lh in enumerate((UPi, LOWs, ones)):
                    E("pe", "matmul", [SCA[gp * 2 + ti], CST], PBK[bank], pbk[bank][:, ti * 12 + i * 4: ti * 12 + (i + 1) * 4], lhsT=lh, rhs=sc2[:, gp * 2 + ti, 12:16], start=True, stop=True)
            E("act", "activation", PBK[bank], SCg, out=sc2[:, gp * 2: gp * 2 + ntl, 16:28], in_=pbk[bank][:, 0:ntl * 12].rearrange("p (t c) -> p t c", t=ntl), func=AF.Exp)
            yield
            if fresh_ctx:
                E("pool", "memset", [], PCB, pcv_all[:, :, 0:3], 0.0)
            else:
                E("pool", "tensor_copy", [CCTX], PCB, out=pcv_all[:, :, 0:3], in_=cctx[:])
            for jp in range(6):
                bank = getbank()
                rs = wload(("a", jp), wa_s[jp])
                wv = wring[rs][:].rearrange("p (k n) -> p k n", k=8)
                for half in range(2):
                    j = jp * 2 + half
                    pst = pbk[bank][:, half * 256: half * 256 + N]
                    for k in range(8):
                        E("pe", "matmul", HTg + [WRING[rs]], PBK[bank], pst, lhsT=wv[:, k, half * 128:(half + 1) * 128], rhs=hT[:, k, 0:N], start=(k == 0), stop=(k == 7))
                yield
                for half in range(2):
                    j = jp * 2 + half
                    pst = pbk[bank][:, half * 256: half * 256 + N]
                    E("act", "copy", PBK[bank], [PCB[j]], out=pc(j, 3, 3 + N), in_=pst)
                yield
            ginfo[g] = (sl, N)

        def run_group_rest(g, b, t0, ntl, is_sample, sidx, nkf, last):
            sl, N = ginfo[g]
            gp = g % 2
            HTg = HT[0:ntl]
            SCg = SCA[gp * 2: gp * 2 + ntl]
            SCBg = SCN[gp * 2: gp * 2 + ntl]
            gdn_decay(0, 0, gp=gp)
            yield
            gdn_decay(0, 1, gp=gp)
            yield
            mixer_inputs = []
            for ti in range(ntl):
                def bproj(c0, c1, ti=ti, banks=None):
                    bank = next_full() if banks is None else banks.pop(0)
                    pst = pbk[bank][:, 0:c1 - c0]
                    for k in range(8):
                        E("pe", "matmul", [HT[ti], WIN[k]], PBK[bank], pst, lhsT=hT[:, k, ti * 128:(ti + 1) * 128], rhs=win_sb[:, k, c0 - WB0:c1 - WB0], start=(k == 0), stop=(k == 7))
                    return bank, pst
                mixer_inputs.append(bproj)
            ginfo[g] = (sl, N, mixer_inputs)
            tile_front(b, 0, t0, mixer_inputs[0], is_sample, sidx, nkf)
            yield
            pend = None

            def conv_final(j):
                a_, t_ = cacc[j % 2], ctnh[j % 2]
                A_, T_ = CACC[j % 2], CTNH[j % 2]
                E("dve", "scalar_tensor_tensor", [A_, T_], [QKB[j]], out=qkvT(j, 0, N), in0=t_[:, 0:N], scalar=1.0, in1=a_[:, 0:N], op0=ALU.add, op1=ALU.mult)
                if j < 8:
                    E("act", "activation", [QKB[j]], [SQT], out=sqT[:, j, 0:N], in_=qkvT(j, 0, N), func=AF.Square)
            for j in (8, 9, 10, 11, 0, 1, 2, 3, 4, 5, 6, 7):
                a_, t_ = cacc[j % 2], ctnh[j % 2]
                A_, T_ = CACC[j % 2], CTNH[j % 2]
                E("act", "activation", [PCB[j], CW], [A_], out=a_[:, 0:N], in_=pc(j, 0, N), func=AF.Copy, scale=cw[:, j, 0:1])
                for i in range(1, 4):
                    E("dve", "scalar_tensor_tensor", [PCB[j], CW, A_], [A_], out=a_[:, 0:N], in0=pc(j, i, i + N), scalar=cw[:, j, i:i + 1], in1=a_[:, 0:N], op0=ALU.mult, op1=ALU.add)
                E("act", "activation", [A_], [T_], out=t_[:, 0:N], in_=a_[:, 0:N], func=AF.Tanh)
                if pend is not None:
                    conv_final(pend)
                pend = j
                yield
            conv_final(pend)
            bank = next_full()
            for ti in range(ntl):
                for j in range(8):
                    E("pe", "matmul", [SQT, ONB], PBK[bank], pbk[bank][:, ti * 8 + j: ti * 8 + j + 1], lhsT=sqT[:, j, ti * 128:(ti + 1) * 128], rhs=onesb[:, 0:1], start=True, stop=True)
            E("dve", "tensor_scalar", PBK[bank], SCBg, out=sc2[:, gp * 2: gp * 2 + ntl, 0:8], in0=pbk[bank][:, 0:ntl * 8].rearrange("p (t c) -> p t c", t=ntl), scalar1=EPS, scalar2=None, op0=ALU.add)
            E("pool", "tensor_tensor", SCBg + [MISC], SCBg, out=sc2[:, gp * 2: gp * 2 + ntl, 0:8], in0=sc2[:, gp * 2: gp * 2 + ntl, 0:8], in1=ap3(misc, 128, [[0, ntl], [0, 8]]), op=ALU.pow)
            E("dve", "tensor_scalar", SCBg, SCBg, out=sc2[:, gp * 2: gp * 2 + ntl, 0:4], in0=sc2[:, gp * 2: gp * 2 + ntl, 0:4], scalar1=128.0 ** -0.5, scalar2=None, op0=ALU.mult)
            E("dve", "tensor_tensor", SCg + SCBg, SCBg, out=sc2[:, gp * 2: gp * 2 + ntl, 28:32], in0=sc2[:, gp * 2: gp * 2 + ntl, 4:8], in1=sc2[:, gp * 2: gp * 2 + ntl, 8:12], op=ALU.mult)
            E("dve", "tensor_tensor", SCg + SCBg, SCBg, out=sc2[:, gp * 2: gp * 2 + ntl, 28:32], in0=sc2[:, gp * 2: gp * 2 + ntl, 28:32], in1=sc2[:, gp * 2: gp * 2 + ntl, 16:20], op=ALU.mult)
            E("dve", "tensor_tensor", SCg + SCBg, SCBg, out=sc2[:, gp * 2: gp * 2 + ntl, 32:36], in0=sc2[:, gp * 2: gp * 2 + ntl, 4:8], in1=sc2[:, gp * 2: gp * 2 + ntl, 20:24], op=ALU.mult)
            finish_conv_ctx(b, N, last, is_sample, sidx)

        def finish_conv_ctx(b, N, last, is_sample, sidx):
            pcv = R1[:, 0:12 * 259].rearrange("p (j n) -> p j n", j=12)
            if last:
                c0 = 16 if is_sample else N
                dsrc = (conv_s[sidx] if is_sample else conv_p[b])
                for r in range(3):
                    for j in range(12):
                        P.stores.append(P.add("sp", lambda e, r=r, j=j: e.dma_start(out=dsrc[r, j * 128:(j + 1) * 128].rearrange("(p o) -> p o", o=1),
                                                                                   in_=pcv[:, j, c0 + r:c0 + r + 1], allow_slow_non_contiguous=True),
                                              reads=[PCB[j]], dma_buf=PCB[j]))
            else:
                E("pool", "tensor_copy", PCB, [CCTX], out=cctx[:], in_=pcv[:, :, N:N + 3])

        def tile_front(b, ti, tseq, bproj, is_sample, sidx, n_keep_from, banks=None):
            for i, (c0, c1) in enumerate(((QB_, KB_), (KB_, VB_), (VB_, IN_COLS))):
                bank, pst = bproj(c0, c1, banks=banks)
                E("act" if i != 1 else "dve", "copy" if i != 1 else "tensor_copy", PBK[bank], [BSB[i]], out=bsb[:, i * 512:(i + 1) * 512], in_=pst)
            bank, pst = bproj(GA_, BA_, banks=banks)
            gsil, GSIL = gsil2[ti], GSIL2[ti]
            E("act", "activation", PBK[bank], [GSIL], out=gsil[:], in_=pst, func=AF.Tanh, scale=0.5)
            E("dve", "scalar_tensor_tensor", PBK[bank] + [GSIL], [GSIL], out=gsil[:], in0=gsil[:], scalar=1.0, in1=pst, op0=ALU.add, op1=ALU.mult)
            if is_sample:
                P.stores.append(P.add("sp", lambda e: e.dma_start(out=vn_s[sidx], in_=bsb[0:16, 1024:1536]), reads=[BSB[2]], dma_buf=BSB[2]))
            elif tseq >= n_keep_from:
                r0 = (tseq - n_keep_from) * 128
                P.stores.append(P.add("sp", lambda e: e.dma_start(out=vb_p[b, r0:r0 + 128, :], in_=bsb[:, 1024:1536]), reads=[BSB[2]], dma_buf=BSB[2]))

        def tile_body(b, ti, tseq, tglob, slot_of, is_sample, sidx, n_keep_from, hook=None, gp=0):
            g1 = gdn_tile(b, ti, is_sample, gp)
            g2 = attn_tile(b, ti, tglob, slot_of)
            next(g1)
            next(g2)
            if is_sample:
                P.stores.append(P.add("sp", lambda e: e.dma_start(out=kn_s[sidx], in_=bsb[0:16, 512:1024]), reads=[BSB[1]], dma_buf=BSB[1]))
            elif tseq >= n_keep_from:
                r0 = (tseq - n_keep_from) * 128
                P.stores.append(P.add("sp", lambda e: e.dma_start(out=kb_p[b, r0:r0 + 128, :], in_=bsb[:, 512:1024]), reads=[BSB[1]], dma_buf=BSB[1]))
            alive = [g1, g2]
            rnd = 0
            while alive:
                rnd += 1
                if hook is not None and rnd == hook[0]:
                    hook[1]()
                    hook = None
                for gen in list(alive):
                    if gen is g2 and rnd <= 2 and g1 in alive:
                        continue
                    try:
                        next(gen)
                    except StopIteration:
                        alive.remove(gen)
                yield
            if hook is not None:
                hook[1]()

        def tile_back(ti):
            trb = pbk[TR].bitcast(BF16)
            mix_bf = mixb2[ti]
            for j in range(8):
                E("pe", "transpose", [MIXA2[ti], MIXB2[ti][0], MIXB2[ti][1], IDB], PBK[TR], out=trb[:, j * 128:(j + 1) * 128], in_=mix_bf[:, j * 128:(j + 1) * 128], identity=identb[:])
            E("act", "copy", PBK[TR], [MIXT[ti]], out=mixT[:, :, ti * 128:(ti + 1) * 128], in_=trb[:, 0:1024].rearrange("p (a b) -> p a b", a=8))

        def group_tail(b, sl, ntl, t0, is_sample, sidx):
            N = 128 * ntl
            wst = {}

            def mm_out(dc, pst, wr):
                if dc % 2 == 0:
                    wst["rs"] = wload(("o", dc // 2), wo_s[dc // 2])
                rs = wst["rs"]
                wv = wring[rs][:].rearrange("p (k n) -> p k n", k=8)
                for k in range(8):
                    E("pe", "matmul", MIXT[0:ntl] + [WRING[rs]], wr, pst, lhsT=wv[:, k, (dc % 2) * 128:(dc % 2 + 1) * 128], rhs=mixT[:, k, 0:N], start=(k == 0), stop=(k == 7))
                return
                yield
            yield from dense_back(sl, ntl, b, 2, mm_out)
            for ti in range(ntl):
                norm_to_hT(sl, ti, 1, b)
                yield
            for blk in range(16):
                rs = wload(("u", blk), wu_s[blk])
                wv = wring[rs][:].rearrange("p (k n) -> p k n", k=8)
                bank = tbank()
                for s2 in range(2):
                    pst = pbk[bank][:, s2 * 256: s2 * 256 + N]
                    for k in range(8):
                        E("pe", "matmul", H2T[0:ntl] + [WRING[rs]], PBK[bank], pst, lhsT=wv[:, k, s2 * 128:(s2 + 1) * 128], rhs=h2T[:, k, 0:N], start=(k == 0), stop=(k == 7))
                for s2 in range(2):
                    pst = pbk[bank][:, s2 * 256: s2 * 256 + N]
                    f = blk * 2 + s2
                    fs = f % 2
                    E("act", "activation", PBK[bank], [FFT[fs]], out=fft[fs][:, 0:N], in_=pst, func=AF.Relu)
                    E("dve" if s2 == 0 else "pool", "tensor_tensor", [FFT[fs]], [FFB], out=ffT(f, 0, N), in0=fft[fs][:, 0:N], in1=fft[fs][:, 0:N], op=ALU.mult)
                yield

            def mm_down(dc, pst, wr):
                for fh in range(2):
                    rs = wload(("d", dc * 2 + fh), wd_s[dc * 2 + fh])
                    wv = wring[rs][:].rearrange("p (f n) -> p f n", f=16)
                    for f16 in range(16):
                        f = fh * 16 + f16
                        E("pe", "matmul", [FFB, WRING[rs]], wr, pst, lhsT=wv[:, f16, :], rhs=ffT(f, 0, N), start=(f == 0), stop=(f == 31))
                    if fh == 0:
                        yield
            yield from dense_back(sl, ntl, b, 5, mm_down)
            for ti in range(ntl):
                if is_sample:
                    P.stores.append(P.add("sp", lambda e, ti=ti: e.dma_start(out=y_s[sidx], in_=xg[sl][0:16, ti, :]), reads=[XG[sl][ti]], dma_buf=XG[sl][ti]))
                else:
                    r0 = (t0 + ti) * 128
                    P.stores.append(P.add("sp", lambda e, ti=ti, r0=r0: e.dma_start(out=y_p[b, r0:r0 + 128, :], in_=xg[sl][:, ti, :]), reads=[XG[sl][ti]], dma_buf=XG[sl][ti]))

        groups = []
        ntl_p = 2 if NT % 2 == 0 else 1
        for b in range(n_pseq):
            for t0 in range(0, NT, ntl_p):
                groups.append(dict(b=b, t0=t0, ntl=ntl_p, sample=False, sidx=0, first=(t0 == 0), last=(t0 + ntl_p >= NT)))
        for s_ in range(n_sseq):
            groups.append(dict(b=n_pseq + s_, t0=0, ntl=1, sample=True, sidx=s_, first=True, last=True))
        def state_init(g):
            G = groups[g]
            is_sample, sidx = G["sample"], G["sidx"]
            if G["first"] and not is_sample:
                E("pool", "memset", [], [CCTX], cctx[:], 0.0)
                E("pool", "memset", [], [SB32], S32[:], 0.0)
                E("pool", "memset", [], [SBF], Sbf[:], 0.0)
            if is_sample:
                s_ = sidx
                for r in range(3):
                    for j in range(12):
                        DMA("sp", cctx[:, j, r:r + 1], sconv[s_][r, j * 128:(j + 1) * 128].rearrange("(p o) -> p o", o=1), [], [CCTX], CCTX, allow_slow_non_contiguous=True)
                DMA("sp", S32[:].rearrange("p (h e) -> p h e", h=4), sgdn[s_].rearrange("h d e -> d h e"), [], [SB32], SB32)
                E("act", "copy", [SB32], [SBF], out=Sbf[:], in_=S32[:])
                so = 1 - (g % 2)
                stg = xg[so][:].rearrange("p a b -> p (a b)")
                trb = pbk[TR].bitcast(BF16)
                DMA("sp", stg.rearrange("p (t n) -> p t n", t=4), ckd[s_].rearrange("(t p) n -> p t n", p=128), [], XG[so], XG[so][0])
                for t in range(4):
                    E("pool", "tensor_copy", XG[so], [QKN], out=qkn_bf[:, 0:512], in_=stg[:, t * 512:(t + 1) * 512])
                    for j in range(4):
                        E("pe", "transpose", [QKN, IDB], PBK[TR], out=trb[:, j * 128:(j + 1) * 128], in_=qkn_bf[:, j * 128:(j + 1) * 128], identity=identb[:])
                    E("dve", "tensor_copy", PBK[TR], [KRING[t]], out=kring[:, :, t * 128:(t + 1) * 128], in_=trb[:, 0:512].rearrange("p (a b) -> p a b", a=4))
                DMA("sp", stg.rearrange("p (t n) -> p t n", t=4), cvd[s_].rearrange("(t p) n -> p t n", p=128), [], XG[so], XG[so][0])
                for t in range(4):
                    E("pool", "memset", [], [VRING[t]], vring[:, t, :, 64:65], 1.0)
                    E("pool", "tensor_copy", XG[so], [VRING[t]], out=vring[:, t, :, 0:64], in_=stg[:, t * 512:(t + 1) * 512].rearrange("p (h e) -> p h e", h=8))

        def early_gen(g, under_mixers):
            G = groups[g]
            nkf = 0 if G["sample"] else NT - KEEP
            fresh = G["first"] and not G["sample"]
            getbank = (lambda: SC1) if under_mixers else next_full
            yield from run_group_early(g, G["b"], G["t0"], G["ntl"], G["sample"], G["sidx"], nkf, getbank, fresh)

        def rest_gen(g):
            G = groups[g]
            nkf = 0 if G["sample"] else NT - KEEP
            yield from run_group_rest(g, G["b"], G["t0"], G["ntl"], G["sample"], G["sidx"], nkf, G["last"])

        def step(gen):
            try:
                next(gen)
                return True
            except StopIteration:
                return False

        def exhaust(gen):
            for _ in gen:
                pass

        def can_pipe(g):
            return g + 1 < len(groups) and not groups[g]["sample"] and not groups[g + 1]["sample"]

        flags = {}

        def mixers_gen(g):
            G = groups[g]
            b, t0, ntl, is_sample, sidx = G["b"], G["t0"], G["ntl"], G["sample"], G["sidx"]
            sl, N, bprojs = ginfo[g]
            gp = g % 2
            nkf = 0 if is_sample else NT - KEEP
            slot_of = (lambda t: t) if is_sample else (lambda t: t % RING)
            for ti in range(ntl):
                tseq = t0 + ti
                hook = None
                if ti + 1 < ntl:
                    def nxt(ti=ti, tseq=tseq):
                        gdn_decay(ti + 1, 0, gp=gp)
                        tile_front(b, ti + 1, tseq + 1, bprojs[ti + 1], is_sample, sidx, nkf, banks=[SC0, PV, SC0, PV])
                        gdn_decay(ti + 1, 1, bank=SC0, gp=gp)
                        flags[("hT_free", g)] = True
                    hook = (15, nxt)
                elif ti > 0:
                    hook = (4, lambda ti=ti: tile_back(ti - 1))
                yield from tile_body(b, ti, tseq, 4 if is_sample else tseq, slot_of, is_sample, sidx, nkf, hook, gp)
            flags[("hT_free", g)] = True
            tile_back(ntl - 1)
            if G["last"]:
                dstS = (gdn_s[sidx] if is_sample else gdn_p[b])
                P.stores.append(P.add("sp", lambda e, dstS=dstS: e.dma_start(out=dstS.rearrange("h d e -> d h e"), in_=S32[:].rearrange("p (h e) -> p h e", h=4)), reads=[SB32], dma_buf=SB32))

        def load_group_x(g):
            Gn = groups[g]
            load_x(g, Gn["b"], Gn["t0"], Gn["ntl"], Gn["sample"], Gn["sidx"])

        load_group_x(0)
        state_init(0)
        exhaust(early_gen(0, False))
        exhaust(rest_gen(0))
        tprev = None
        for g, G in enumerate(groups):
            mg = mixers_gen(g)
            pipe = can_pipe(g)
            eg = None
            x_loaded = False
            if pipe and tprev is None:
                load_group_x(g + 1)
                x_loaded = True
            ma = True
            while ma:
                ma = step(mg)
                if tprev is not None and not step(tprev):
                    tprev = None
                    if pipe:
                        load_group_x(g + 1)
                        x_loaded = True
                if pipe and eg is None and x_loaded and flags.get(("hT_free", g)):
                    eg = early_gen(g + 1, True)
                if eg is not None and eg is not False:
                    if not step(eg):
                        eg = False
            if tprev is not None:
                exhaust(tprev)
                tprev = None
            tg = group_tail(G["b"], ginfo[g][0], G["ntl"], G["t0"], G["sample"], G["sidx"])
            if g + 1 < len(groups):
                if pipe:
                    if not x_loaded:
                        load_group_x(g + 1)
                    if eg is None:
                        eg = early_gen(g + 1, False)
                    state_init(g + 1)
                    fa = True
                    if eg is not False:
                        while step(eg):
                            step(tg)
                    rg = rest_gen(g + 1)
                    tmode["fast"] = True
                    while step(rg):
                        step(tg)
                        step(tg)
                    tmode["fast"] = False
                    tprev = tg
                else:
                    exhaust(tg)
                    load_group_x(g + 1)
                    state_init(g + 1)
                    exhaust(early_gen(g + 1, False))
                    exhaust(rest_gen(g + 1))
            else:
                exhaust(tg)

        print('SBUF bytes remaining', nc.sbuf_bytes_remaining() if callable(nc.sbuf_bytes_remaining) else nc.sbuf_bytes_remaining)
        P.cut = False
        P.add("sp", lambda e: e.nop(), extra_deps=[x for x in P.stores if x is not None])
        stats = P.emit()
        build_nc.stats = stats
    return nc


def _consts():
    i = np.arange(128)
    ident = np.eye(128, dtype=np.float32)
    UPi = (i[:, None] <= i[None, :]).astype(np.float32)
    LOWs = (i[:, None] > i[None, :]).astype(np.float32)
    LOWi = (i[:, None] >= i[None, :]).astype(np.float32)
    ones = np.ones((128, 128), np.float32)
    return np.ascontiguousarray(np.stack([ident, UPi, LOWs, LOWi, ones], axis=1))


def _gmask():
    i = np.arange(128)
    ms = [((i[:, None] // 8) == (i[None, :] // 8)) & (i[:, None] > i[None, :])]
    for sz in (8, 16, 32, 64):
        ms.append(((i[:, None] // (2 * sz)) == (i[None, :] // (2 * sz))) & ((i[:, None] // sz) == (i[None, :] // sz) + 1))
    return np.ascontiguousarray(np.stack([m.astype(np.float32) for m in ms], axis=1))


def _mask():
    m = np.zeros((128, 5, 128), np.float32)
    m[0:64, 4, 64:128] = NEG
    m[64:128, 0, 0:64] = NEG
    return m


def _bias_index():
    ki = np.arange(128)[:, None, None]
    dt = np.arange(5)[None, :, None]
    qi = np.arange(128)[None, None, :]
    d = np.clip(dt * 128 + qi - ki, -63, 256) + 63
    return d


_NC_CACHE = {}


def kernel(x_prompt, x_sample, state_conv, state_gdn, cache_k_band, cache_v_band, c_prompt, c_sample,
           w_mod, b_mod, norm1_g, norm2_g, w_in, conv_w, a_log, dt_bias, gdn_norm_g, qn_g, kn_g,
           rel_bias, w_out, w_up, w_down):
    f = lambda a: np.ascontiguousarray(np.asarray(a, dtype=np.float32))
    x_prompt, x_sample = f(x_prompt), f(x_sample)
    B, T, _ = x_prompt.shape
    SB = x_sample.shape[0]
    npq, nsq = B // NCORES, SB // NCORES
    key = (npq, T, nsq)
    if key not in _NC_CACHE:
        _NC_CACHE[key] = build_nc(npq, T, nsq)
    nc = _NC_CACHE[key]
    idx = _bias_index()
    rb = f(rel_bias)[0]
    biasT = np.ascontiguousarray(np.transpose(rb[:, idx], (1, 2, 0, 3)))
    shared = {
        "w_mod": f(w_mod)[0], "bmodT": np.ascontiguousarray(f(b_mod)[0].reshape(48, 128).T),
        "n1gT": np.ascontiguousarray(f(norm1_g)[0].reshape(8, 128).T), "n2gT": np.ascontiguousarray(f(norm2_g)[0].reshape(8, 128).T),
        "w_in": f(w_in)[0], "cwT": np.ascontiguousarray(np.transpose(f(conv_w)[0].reshape(4, 12, 128), (2, 1, 0))),
        "alog": f(a_log).reshape(1, 4), "dtb": f(dt_bias).reshape(1, 4), "gng": f(gdn_norm_g).reshape(1, 128),
        "qkg": np.ascontiguousarray(np.concatenate([f(qn_g)[0], f(kn_g)[0]]).reshape(1, 128)),
        "biasT": biasT, "maskT": _mask(), "w_out": f(w_out)[0], "w_up": f(w_up)[0], "w_down": f(w_down)[0],
        "consts": _consts(), "valid16": (np.arange(128) < 16).astype(np.float32).reshape(128, 1), "gmask": _gmask(),
    }
    sc, sg = f(state_conv)[0], f(state_gdn)[0]
    ck, cv = f(cache_k_band)[0].reshape(SB, 512, 512), f(cache_v_band)[0].reshape(SB, 512, 512)
    cp, cs = f(c_prompt), f(c_sample)
    in_maps = []
    for c in range(NCORES):
        ps, ss = slice(c * npq, (c + 1) * npq), slice(c * nsq, (c + 1) * nsq)
        cvec = np.concatenate([cp[ps], cs[ss]], axis=0)
        cT = np.ascontiguousarray(np.transpose(cvec.reshape(npq + nsq, 8, 128), (2, 1, 0)))
        m = dict(shared)
        m.update({"xp": x_prompt[ps], "xs": x_sample[ss], "sconv": sc[ss], "sgdn": sg[ss], "ck": ck[ss], "cv": cv[ss], "cT": cT})
        in_maps.append(m)
    res = run_bass_kernel_spmd(nc, in_maps, core_ids=list(range(NCORES)))
    R = res.results
    cat = lambda k: np.concatenate([np.asarray(r[k], dtype=np.float32) for r in R], axis=0)
    keep = min(512, T)
    return (cat("y_p"), cat("y_s"), cat("conv_p")[None], cat("gdn_p")[None],
            cat("kb_p").reshape(1, B, keep, 8, 64), cat("vb_p").reshape(1, B, keep, 8, 64),
            cat("conv_s")[None], cat("gdn_s")[None],
            cat("kn_s").reshape(1, SB, 16, 8, 64), cat("vn_s").reshape(1, SB, 16, 8, 64))
```
